# Optimizing a Trainium2 kernel written in Bass

```python
import math
import jax, jax.numpy as jnp
from jax import lax
import numpy as np

D_MODEL = 2048
BATCH = 2
SEQ = 4096
DEPTH = 1

CHUNK = 64
N_META = 16
Q_BLOCK = 128
ATTN_WIDTH = D_MODEL // 2
CONV_WIDTH = D_MODEL // 2
MIX_WIDTH = ATTN_WIDTH + CONV_WIDTH
DA_HEAD_DIM = 128
DA_N_HEADS = ATTN_WIDTH // DA_HEAD_DIM
DA_QK_DIM = DA_HEAD_DIM // 2
CONV_KERNEL = 31
D_FF = 5632
PROJ_WIDTH = 3 * ATTN_WIDTH + 2 * CONV_WIDTH
EPS = 1e-6
NEG_INF = -1e30

kernel_name = "hybrid_diffattn_conformer_conv_macaron"


def lambda_init_fn(layer_idx):
    return 0.8 - 0.6 * math.exp(-0.3 * layer_idx)


def rms_norm(x, g):
    xf = x.astype(jnp.float32)
    y = xf * lax.rsqrt(jnp.mean(xf * xf, axis=-1, keepdims=True) + EPS)
    return (y * g.astype(jnp.float32)).astype(x.dtype)


def swiglu(h, w_gate, w_up, w_down):
    return (jax.nn.silu(h @ w_gate) * (h @ w_up)) @ w_down


def chunk_ids(length):
    pos = jnp.arange(length)
    return jnp.where(pos < N_META, 0, 1 + (pos - N_META) // CHUNK)


def diff_attention(q, k, v, lam, lam_init, subln_g):
    B, L, H = q.shape[0], q.shape[1], q.shape[2]
    n_blk = -(-L // Q_BLOCK)
    Lp = n_blk * Q_BLOCK
    pad = Lp - L
    qp = jnp.pad(q, ((0, 0), (0, pad), (0, 0), (0, 0), (0, 0)))
    kp = jnp.pad(k, ((0, 0), (0, pad), (0, 0), (0, 0), (0, 0)))
    vp = jnp.pad(v, ((0, 0), (0, pad), (0, 0), (0, 0)))
    cid = chunk_ids(Lp)
    qb = qp.reshape(B, n_blk, Q_BLOCK, H, 2, DA_QK_DIM).transpose(1, 0, 2, 3, 4, 5)
    qcid = cid.reshape(n_blk, Q_BLOCK)
    scale = DA_QK_DIM ** -0.5

    def block(args):
        q_blk, cq = args
        s = jnp.einsum('bqhcd,bkhcd->bhcqk', q_blk, kp).astype(jnp.float32) * scale
        visible = cid[None, :] <= cq[:, None]
        s = jnp.where(visible[None, None, None], s, NEG_INF)
        p = jax.nn.softmax(s, axis=-1)
        attn = p[:, :, 0] - lam * p[:, :, 1]
        return jnp.einsum('bhqk,bkhd->bqhd', attn.astype(vp.dtype), vp)

    o = lax.map(block, (qb, qcid))
    o = o.transpose(1, 0, 2, 3, 4).reshape(B, Lp, H, DA_HEAD_DIM)[:, :L]
    o = rms_norm(o, subln_g) * (1.0 - lam_init)
    return o.reshape(B, L, ATTN_WIDTH)


def conformer_conv(a, gate, conv_w, conv_b, norm_g):
    u = a * jax.nn.sigmoid(gate)
    y = lax.conv_general_dilated(
        u, conv_w.astype(u.dtype), window_strides=(1,), padding=[(CONV_KERNEL - 1, 0)],
        dimension_numbers=('NWC', 'WIO', 'NWC'), feature_group_count=CONV_WIDTH)
    y = y + conv_b
    y = rms_norm(y, norm_g)
    return jax.nn.silu(y)


def setup_inputs(seed: int = 0) -> dict:
    key = jax.random.key(seed)
    ks = jax.random.split(key, 24)
    f32 = jnp.float32

    def nrm(k, shape, fan_in):
        return jax.random.normal(k, shape, f32) * (fan_in ** -0.5)

    def gain(k, shape):
        return 1.0 + 0.01 * jax.random.normal(k, shape, f32)

    return {
        "x": jax.random.normal(ks[0], (BATCH, SEQ, D_MODEL), f32),
        "meta_tokens": jax.random.normal(ks[1], (N_META, D_MODEL), f32),
        "ffn1_norm_g": gain(ks[2], (DEPTH, D_MODEL)),
        "ffn1_w_gate": nrm(ks[3], (DEPTH, D_MODEL, D_FF), D_MODEL),
        "ffn1_w_up": nrm(ks[4], (DEPTH, D_MODEL, D_FF), D_MODEL),
        "ffn1_w_down": nrm(ks[5], (DEPTH, D_FF, D_MODEL), D_FF),
        "mix_norm_g": gain(ks[6], (DEPTH, D_MODEL)),
        "w_in": nrm(ks[7], (DEPTH, D_MODEL, PROJ_WIDTH), D_MODEL),
        "q_norm_g": gain(ks[8], (DEPTH, DA_QK_DIM)),
        "k_norm_g": gain(ks[9], (DEPTH, DA_QK_DIM)),
        "lambda_q1": 0.1 * jax.random.normal(ks[10], (DEPTH, DA_QK_DIM), f32),
        "lambda_k1": 0.1 * jax.random.normal(ks[11], (DEPTH, DA_QK_DIM), f32),
        "lambda_q2": 0.1 * jax.random.normal(ks[12], (DEPTH, DA_QK_DIM), f32),
        "lambda_k2": 0.1 * jax.random.normal(ks[13], (DEPTH, DA_QK_DIM), f32),
        "attn_subln_g": gain(ks[14], (DEPTH, DA_HEAD_DIM)),
        "conv_w": nrm(ks[15], (DEPTH, CONV_KERNEL, 1, CONV_WIDTH), CONV_KERNEL),
        "conv_b": 0.01 * jax.random.normal(ks[16], (DEPTH, CONV_WIDTH), f32),
        "conv_norm_g": gain(ks[17], (DEPTH, CONV_WIDTH)),
        "w_out": nrm(ks[18], (DEPTH, MIX_WIDTH, D_MODEL), MIX_WIDTH),
        "ffn2_norm_g": gain(ks[19], (DEPTH, D_MODEL)),
        "ffn2_w_gate": nrm(ks[20], (DEPTH, D_MODEL, D_FF), D_MODEL),
        "ffn2_w_up": nrm(ks[21], (DEPTH, D_MODEL, D_FF), D_MODEL),
        "ffn2_w_down": nrm(ks[22], (DEPTH, D_FF, D_MODEL), D_FF),
        "final_norm_g": gain(ks[23], (DEPTH, D_MODEL)),
    }


def reference(x, meta_tokens, ffn1_norm_g, ffn1_w_gate, ffn1_w_up, ffn1_w_down,
              mix_norm_g, w_in, q_norm_g, k_norm_g, lambda_q1, lambda_k1, lambda_q2,
              lambda_k2, attn_subln_g, conv_w, conv_b, conv_norm_g, w_out,
              ffn2_norm_g, ffn2_w_gate, ffn2_w_up, ffn2_w_down, final_norm_g):
    B = x.shape[0]
    meta = jnp.broadcast_to(meta_tokens[None].astype(x.dtype), (B, N_META, D_MODEL))
    h_stream = jnp.concatenate([meta, x], axis=1)
    L = h_stream.shape[1]
    split_pts = [ATTN_WIDTH, 2 * ATTN_WIDTH, 3 * ATTN_WIDTH, 3 * ATTN_WIDTH + CONV_WIDTH]

    for l in range(DEPTH):
        lam_init = lambda_init_fn(l + 1)
        h = rms_norm(h_stream, ffn1_norm_g[l])
        h_stream = h_stream + 0.5 * swiglu(h, ffn1_w_gate[l], ffn1_w_up[l], ffn1_w_down[l])

        h = rms_norm(h_stream, mix_norm_g[l])
        proj = h @ w_in[l]
        q, k, v, ca, cg = jnp.split(proj, split_pts, axis=-1)
        q = rms_norm(q.reshape(B, L, DA_N_HEADS, 2, DA_QK_DIM), q_norm_g[l])
        k = rms_norm(k.reshape(B, L, DA_N_HEADS, 2, DA_QK_DIM), k_norm_g[l])
        v = v.reshape(B, L, DA_N_HEADS, DA_HEAD_DIM)
        lam = (jnp.exp(jnp.sum(lambda_q1[l].astype(jnp.float32) * lambda_k1[l].astype(jnp.float32)))
               - jnp.exp(jnp.sum(lambda_q2[l].astype(jnp.float32) * lambda_k2[l].astype(jnp.float32)))
               + lam_init)
        o_att = diff_attention(q, k, v, lam, lam_init, attn_subln_g[l])
        o_conv = conformer_conv(ca, cg, conv_w[l], conv_b[l], conv_norm_g[l])
        h_stream = h_stream + jnp.concatenate([o_att, o_conv], axis=-1) @ w_out[l]

        h = rms_norm(h_stream, ffn2_norm_g[l])
        h_stream = h_stream + 0.5 * swiglu(h, ffn2_w_gate[l], ffn2_w_up[l], ffn2_w_down[l])
        h_stream = rms_norm(h_stream, final_norm_g[l])

    return h_stream[:, N_META:]
```

```python
import numpy as np
from contextlib import ExitStack

import concourse.bass as bass
import concourse.mybir as mybir
from concourse.bass_utils import run_bass_kernel_spmd

F32 = mybir.dt.float32
BF16 = mybir.dt.bfloat16
ALU = mybir.AluOpType
AF = mybir.ActivationFunctionType

ENGS = ("pe", "act", "dve", "pool", "sp")
SEM_LIM = 3000


class Op:
    __slots__ = ("eng", "fn", "deps", "chan", "chan_val", "needed", "tick", "dma_deps", "inc")


class V:
    __slots__ = ("ap", "key", "lo", "hi")

    def __init__(self, ap, key, lo, hi):
        self.ap, self.key, self.lo, self.hi = ap, key, lo, hi

    def reg(self):
        return (self.key, self.lo, self.hi)


class Buf:
    def __init__(self, handle, name, n, dtype, esz):
        self.h, self.name, self.n, self.dtype, self.esz = handle, name, n, dtype, esz

    def v(self, a=0, b=None, p0=None, p1=None):
        if b is None:
            b = self.n
        if p0 is None:
            ap = self.h[:, a:b]
        else:
            ap = self.h[p0:p1, a:b]
        return V(ap, self.name, a * self.esz, b * self.esz)

    def v3(self, a, b, inner, p0=None, p1=None):
        t = self.v(a, b, p0, p1)
        return V(t.ap.rearrange("p (c t) -> p c t", t=inner), t.key, t.lo, t.hi)


class Prog:
    def __init__(self):
        self.ops = {e: [] for e in ENGS}
        self.hist = {}
        self.chan_cnt = {}
        self.nops = 0

    def add(self, eng, fn, reads=(), writes=(), chan=None, inc=16, nosame=False):
        op = Op()
        op.eng, op.fn, op.chan, op.needed, op.tick = eng, fn, chan, False, 0
        op.deps = []
        op.dma_deps = {}
        op.chan_val = 0
        self._nosame = nosame
        reads = [r.reg() if isinstance(r, V) else r for r in reads]
        writes = [w.reg() if isinstance(w, V) else w for w in writes]
        ps_r = [(k, lo // 2048 * 2048, (hi + 2047) // 2048 * 2048) for (k, lo, hi) in reads if k.startswith("ps_")]
        reads = [r for r in reads if not r[0].startswith("ps_")]
        writes = [(k, lo // 2048 * 2048, (hi + 2047) // 2048 * 2048) if k.startswith("ps_") else (k, lo, hi)
                  for (k, lo, hi) in writes] + ps_r
        for (key, lo, hi) in reads:
            for ent in self.hist.get(key, ()):
                if ent[3] and ent[0] < hi and lo < ent[1]:
                    self._dep(op, ent[2], True)
        for (key, lo, hi) in writes:
            lst = self.hist.setdefault(key, [])
            keep = []
            for ent in lst:
                if ent[0] < hi and lo < ent[1]:
                    self._dep(op, ent[2], ent[3])
                    if lo <= ent[0] and ent[1] <= hi:
                        continue
                keep.append(ent)
            keep.append([lo, hi, op, True])
            self.hist[key] = keep
        for (key, lo, hi) in reads:
            lst = self.hist.setdefault(key, [])
            if chan is None:
                for ent in lst:
                    if (not ent[3]) and ent[0] == lo and ent[1] == hi and ent[2].eng == eng \
                            and ent[2].chan is None:
                        ent[2] = op
                        break
                else:
                    lst.append([lo, hi, op, False])
            else:
                lst.append([lo, hi, op, False])
        if chan is not None:
            self.chan_cnt[chan] = self.chan_cnt.get(chan, 0) + inc
            op.chan_val = self.chan_cnt[chan]
        op.inc = inc
        self.ops[eng].append(op)
        self.nops += 1
        return op

    def _dep(self, op, d, strong):
        if d is op:
            return
        if d.chan is not None:
            cur = self.chan_cnt[d.chan]
            if op.dma_deps.get(d.chan, 0) < cur:
                op.dma_deps[d.chan] = cur
            return
        if d.eng == op.eng:
            if op.chan is None and (d.eng == "pe" or self._nosame):
                return
        d.needed = True
        op.deps.append(d)

    def emit(self, nc, stack):
        eng_sems = {}
        for e in ENGS:
            n_needed = sum(1 for o in self.ops[e] if o.needed)
            k = n_needed // SEM_LIM + 1
            eng_sems[e] = [stack.enter_context(nc.semaphore(f"s_{e}{i}")) for i in range(k)]
            t = 0
            for o in self.ops[e]:
                if o.needed:
                    t += 1
                    o.tick = t
        chan_sems = {c: stack.enter_context(nc.semaphore(f"c_{c}")) for c in self.chan_cnt}
        prog = self

        def replay(ename, eng):
            waited = {}
            for o in prog.ops[ename]:
                wl = {}
                for d in o.deps:
                    si = (d.tick - 1) // SEM_LIM
                    val = d.tick - si * SEM_LIM
                    kk = ("e", d.eng, si)
                    if wl.get(kk, 0) < val:
                        wl[kk] = val
                for c, val in o.dma_deps.items():
                    kk = ("c", c)
                    if wl.get(kk, 0) < val:
                        wl[kk] = val
                for kk, val in wl.items():
                    if waited.get(kk, 0) >= val:
                        continue
                    waited[kk] = val
                    sem = eng_sems[kk[1]][kk[2]] if kk[0] == "e" else chan_sems[kk[1]]
                    eng.wait_ge(sem, val)
                inst = o.fn(eng)
                if o.chan is not None:
                    inst.then_inc(chan_sems[o.chan], o.inc)
                elif o.needed:
                    si = (o.tick - 1) // SEM_LIM
                    inst.then_inc(eng_sems[ename][si], 1)

        block = stack.enter_context(nc.Block())

        @block.tensor
        def _(eng):
            replay("pe", eng)

        @block.scalar
        def _(eng):
            replay("act", eng)

        @block.vector
        def _(eng):
            replay("dve", eng)

        @block.gpsimd
        def _(eng):
            replay("pool", eng)

        @block.sync
        def _(eng):
            replay("sp", eng)


D = 2048
KC = 16
DFF = 5632
FCN = 44
NTOK = 1024
NMETA = 16
T = NTOK + NMETA
NH = 8
EPS = 1e-6
LAM_INIT = 0.8 - 0.6 * float(np.exp(-0.3))
FBLK = [(0, 12), (12, 24), (24, 34), (34, 44)]
WSLOT = 8192
NSLOT = 3
NEG = -30000.0

TG1, TGM, TG2, TGF = 0, 16, 32, 48
TGQ, TGK, TGS = 64, 65, 66
TCB, TGC = 68, 76
TCW = 84
TSEL = TCW + 248
TMB = TSEL + 5
TSA = TMB + 8
TLAM = TSA + 4
TABN = TLAM + 256


class SBuf(Buf):
    def __init__(self, nc, name, off, n, dtype, base):
        esz = 4 if dtype == F32 else 2
        h = nc.alloc_sbuf_tensor_at(name, [128, n], dtype, offset=base + off)
        Buf.__init__(self, h, "sb", n, dtype, esz)
        self.off = off

    def v(self, a=0, b=None, p0=None, p1=None):
        if b is None:
            b = self.n
        ap = self.h[:, a:b] if p0 is None else self.h[p0:p1, a:b]
        return V(ap, "sb", self.off + a * self.esz, self.off + b * self.esz)


class PBuf(Buf):
    def __init__(self, nc, name, n):
        h = nc.alloc_psum_tensor(name, [128, n], F32)
        Buf.__init__(self, h, "ps_" + name, n, F32, 4)


def build_program(debug=None):
    nc = bass.Bass("TRN2", target_bir_lowering=False)
    P = Prog()
    base = (nc.sbuf_base + 63) // 64 * 64
    cap = nc.sbuf_top - base

    d_xT = nc.dram_tensor("xT", [128, KC * T], F32, kind="ExternalInput")
    d_tab = nc.dram_tensor("tab", [128, TABN], F32, kind="ExternalInput")
    d_f1gu = nc.dram_tensor("f1gu", [22, 128, WSLOT], F32, kind="ExternalInput")
    d_f1d = nc.dram_tensor("f1d", [16, 128, 12 * 512], F32, kind="ExternalInput")
    d_f2gu = nc.dram_tensor("f2gu", [22, 128, WSLOT], F32, kind="ExternalInput")
    d_f2d = nc.dram_tensor("f2d", [16, 128, 12 * 512], F32, kind="ExternalInput")
    d_win = nc.dram_tensor("win", [10, 128, WSLOT], F32, kind="ExternalInput")
    d_wout = nc.dram_tensor("wout", [4, 128, WSLOT], F32, kind="ExternalInput")
    d_out = nc.dram_tensor("outT", [128, KC * NTOK], F32, kind="ExternalOutput")
    d_xsp = nc.dram_tensor("xspill", [128, KC * NTOK], F32, kind="Internal")
    d_kin = [nc.dram_tensor(f"k_gin{i}", [4 * 128, NTOK], BF16, kind="Internal") for i in range(2)]
    d_vin = [nc.dram_tensor(f"v_gin{i}", [NTOK, 512], BF16, kind="Internal") for i in range(2)]
    d_hin = nc.dram_tensor("h_gin", [NH * 128, 32], F32, kind="Internal")
    d_kout = [nc.dram_tensor(f"k_gout{i}", [4 * 4 * 128, NTOK], BF16, kind="Internal") for i in range(2)]
    d_vout = [nc.dram_tensor(f"v_gout{i}", [4 * NTOK, 512], BF16, kind="Internal") for i in range(2)]
    d_hout = nc.dram_tensor("h_gout", [4 * NH * 128, 32], F32, kind="Internal")
    d_dbg = None
    if debug is not None:
        d_dbg = nc.dram_tensor("dbg", [128, debug[1]], F32 if debug[2] == "f32" else BF16, kind="ExternalOutput")
    GROUPS = [[0, 1, 2, 3], [4, 5, 6, 7]]

    off = [0]

    def alloc(name, n, dtype):
        esz = 4 if dtype == F32 else 2
        b = SBuf(nc, name, off[0], n, dtype, base)
        off[0] += (n * esz + 63) // 64 * 64
        assert off[0] <= cap, (name, off[0], cap)
        return b

    def alias(name, o, n, dtype):
        return SBuf(nc, name, o, n, dtype, base)

    X = alloc("x", KC * T, F32)
    HT = alloc("hT", KC * T, BF16)
    ACT = alloc("act", 12 * T + 320, BF16)
    WS = [alloc(f"ws{i}", WSLOT, BF16) for i in range(NSLOT)]
    SG = [alloc(f"sg{i}", T, BF16) for i in range(2)]
    RS = alloc("rs", T, F32)
    SQ = [alloc(f"sq{i}", T, BF16) for i in range(2)]
    TAB = alloc("tab", TABN, F32)
    TB2 = alloc("tab2", 96, F32)
    ONES = alloc("ones", 128, BF16)
    BDG = alloc("bdg", 128, BF16)
    KMETA = alloc("kmeta", NH * 128, BF16)
    ONES16 = alloc("ones16", 128, BF16)
    VMETA = alloc("vmeta", 1024, BF16)
    UMETA = alloc("umeta", 8 * 16, F32)
    LAMT = alloc("lamt", 8, F32)
    MISC0 = off[0]
    UW = 1056
    U = alias("u", X.off, 8 * UW, F32)
    QT = alias("qT", X.off + 8 * UW * 4, NH * NTOK, BF16)
    CATC = alias("catc", X.off + 8 * UW * 4 + NH * NTOK * 2, 8 * NTOK, BF16)
    assert 8 * UW * 4 + NH * NTOK * 2 + 8 * NTOK * 2 <= KC * T * 4
    CATA = alias("cata", ACT.off, 8 * NTOK, BF16)
    ATMP = ACT.off + 8 * NTOK * 2
    KB = [alias(f"kb{i}", HT.off + i * 16384, 4 * NTOK, BF16) for i in range(2)]
    VB = [alias(f"vb{i}", HT.off + i * 16384 + 8192, 32 * 128, BF16) for i in range(2)]
    o_ = ACT.off
    KST = [alias(f"kst{i}", o_ + i * 2112, T, BF16) for i in range(2)]
    VST = [alias(f"vst{i}", o_ + 4224 + i * 1024, 512, BF16) for i in range(2)]
    SIG = [alias(f"sig{i}", o_ + 6272 + i * 4160, T, F32) for i in range(2)]
    assert 6272 + 2 * 4160 <= 16384
    EP = [alias(f"ep{i}", ATMP + i * 2048, 512, F32) for i in range(4)]
    assert 4 * 2048 <= 9216
    EP.append(alloc("ep4", 512, F32))
    PT = [alias(f"pt{i}", SG[0].off + i * 1024, 512, BF16) for i in range(4)]
    assert SG[1].off + 2112 - SG[0].off >= 4096
    CACC = alloc("cacc", NTOK, F32)
    CTMP = alloc("ctmp", NTOK, F32)
    HAL = alias("hal", SG[0].off, 4 * 8 * 32, F32)
    MH = alias("mh", ATMP + 8192, 8 * 32, F32)
    QP = alloc("qp", 2 * NTOK, BF16)
    print("SBUF used", off[0], "of", cap)

    BIG = [PBuf(nc, f"big{i}", 1024) for i in range(3)]
    SMALL = PBuf(nc, "small", 512)
    STAT = PBuf(nc, "stat", 512)
    cnt = {"big": 0, "small": 0, "slot": 0}

    def nbig():
        b = BIG[cnt["big"] % 3]
        cnt["big"] += 1
        return b

    def mm(out, lhsT, rhs, start, stop):
        P.add("pe", lambda e: e.matmul(out.ap, lhsT.ap, rhs.ap, start=start, stop=stop),
              reads=[lhsT, rhs], writes=[out])

    def act(out, in_, func, bias=None, scale=1.0, nosame=False):
        rd = [in_] + ([bias] if isinstance(bias, V) else [])
        if bias is None:
            P.add("act", lambda e: e.activation(out.ap, in_.ap, func, scale=scale), reads=rd, writes=[out], nosame=nosame)
        else:
            b = bias.ap if isinstance(bias, V) else bias
            P.add("act", lambda e: e.activation(out.ap, in_.ap, func, bias=b, scale=scale), reads=rd, writes=[out], nosame=nosame)

    def _s(x):
        return x.ap if isinstance(x, V) else x

    def tsc(out, in0, s1, s2, op0, op1=None, eng="dve"):
        rd = [in0] + [s for s in (s1, s2) if isinstance(s, V)]
        if op1 is None:
            P.add(eng, lambda e: e.tensor_scalar(out.ap, in0.ap, _s(s1), None, op0), reads=rd, writes=[out])
        else:
            P.add(eng, lambda e: e.tensor_scalar(out.ap, in0.ap, _s(s1), _s(s2), op0, op1), reads=rd, writes=[out])

    def stt(out, in0, sc, in1, op0, op1, eng="dve"):
        rd = [in0, in1] + ([sc] if isinstance(sc, V) else [])
        P.add(eng, lambda e: e.scalar_tensor_tensor(out.ap, in0.ap, _s(sc), in1.ap, op0, op1), reads=rd, writes=[out])

    def tt(out, in0, in1, op, eng="dve"):
        P.add(eng, lambda e: e.tensor_tensor(out.ap, in0.ap, in1.ap, op), reads=[in0, in1], writes=[out])

    def dma(eng, out, in_, reads, writes, chan):
        P.add(eng, lambda e: e.dma_start(out=out, in_=in_), reads=reads, writes=writes, chan=chan)

    stream = []
    for b, (c0, c1) in enumerate(FBLK):
        for s in range(c0 // 2, c1 // 2):
            stream.append((d_f1gu, s, WSLOT))
        for dg in range(4):
            stream.append((d_f1d, b * 4 + dg, (c1 - c0) * 512))
    W_IN0 = len(stream)
    for s in range(10):
        stream.append((d_win, s, WSLOT))
    W_OUT0 = len(stream)
    for s in range(4):
        stream.append((d_wout, s, WSLOT))
    F2_0 = len(stream)
    for b, (c0, c1) in enumerate(FBLK):
        for s in range(c0 // 2, c1 // 2):
            stream.append((d_f2gu, s, WSLOT))
        for dg in range(4):
            stream.append((d_f2d, b * 4 + dg, (c1 - c0) * 512))
    issued = [0]
    ag_after = {}

    WS3 = alias("ws3", X.off + 8 * 1056 * 4 + NH * NTOK * 2, WSLOT, BF16)
    ring4 = [WS[2], WS[0], WS[1], WS3]
    slot_of = []
    for j in range(len(stream)):
        if W_IN0 <= j < W_IN0 + 10:
            r_ = (j - W_IN0) % 4
            slot_of.append((ring4[r_], "ws3" if r_ == 3 else f"ws{(2, 0, 1)[r_]}"))
        elif j < W_IN0:
            slot_of.append((WS[j % 3], f"ws{j % 3}"))
        else:
            jj = j - (W_IN0 + 10)
            slot_of.append((WS[(jj + 1) % 3], f"ws{(jj + 1) % 3}"))
    assert W_IN0 % 3 == 2, W_IN0

    def wget(i):
        la = 3 if W_IN0 <= i < W_IN0 + 8 else 2
        while issued[0] < min(len(stream), i + la + 1):
            j = issued[0]
            h, idx, ne = stream[j]
            slot, chn = slot_of[j]
            prev = [p for p in range(j) if slot_of[p][0] is slot]
            assert not prev or prev[-1] < i, (i, j, prev[-1])
            dma("pool", slot.v(0, ne).ap, h.ap()[idx][:, 0:ne], [(h.name, idx, idx + 1)], [slot.v(0, ne)], chn)
            issued[0] += 1
            for th in ag_after.pop(j, []):
                th()
        return slot_of[i][0]

    dma("sp", TAB.v().ap, d_tab.ap(), [("tab_d", 0, 1)], [TAB.v()], "tab")
    for ti_, (c0_, c1_) in enumerate(((0, 512), (512, 1024), (1024, T))):
        for k in range(KC):
            dma("sp", X.v(k * T + c0_, k * T + c1_).ap, d_xT.ap()[:, k * T + c0_:k * T + c1_], [("xT_d", 0, 1)],
                [X.v(k * T + c0_, k * T + c1_)], f"xin{ti_}")
    P.add("dve", lambda e: e.memset(ONES.v().ap, 1.0), writes=[ONES.v()])
    P.add("dve", lambda e: e.memset(ONES16.v().ap, 0.0), writes=[ONES16.v()])
    P.add("dve", lambda e: e.memset(ONES16.v(0, 128, 0, 16).ap, 1.0), writes=[ONES16.v()])
    P.add("dve", lambda e: e.memset(QP.v().ap, 0.0), writes=[QP.v()])
    P.add("dve", lambda e: e.memset(KMETA.v().ap, 0.0), writes=[KMETA.v()])
    P.add("dve", lambda e: e.memset(VMETA.v().ap, 0.0), writes=[VMETA.v()])
    P.add("dve", lambda e: e.memset(BDG.v().ap, 0.0), writes=[BDG.v()])
    P.add("dve", lambda e: e.memset(BDG.v(0, 64, 0, 64).ap, 1.0), writes=[BDG.v()])
    P.add("dve", lambda e: e.memset(BDG.v(64, 128, 64, 128).ap, 1.0), writes=[BDG.v()])
    tsc(TB2.v(0, 64), TAB.v(0, 64), float(np.sqrt(D)), None, ALU.mult)
    tsc(TB2.v(64, 65), TAB.v(TGQ, TGQ + 1), 1.0, None, ALU.mult)
    tsc(TB2.v(65, 66), TAB.v(TGK, TGK + 1), 8.0, None, ALU.mult)
    tsc(TB2.v(66, 67), TAB.v(TGS, TGS + 1), float(np.sqrt(128.0) * (1.0 - LAM_INIT)), None, ALU.mult)
    tsc(TB2.v(68, 76), TAB.v(TGC, TGC + 8), 32.0, None, ALU.mult)
    tt(EP[0].v(0, 64), TAB.v(TLAM, TLAM + 64), TAB.v(TLAM + 64, TLAM + 128), ALU.mult)
    tt(EP[0].v(64, 128), TAB.v(TLAM + 128, TLAM + 192), TAB.v(TLAM + 192, TLAM + 256), ALU.mult)
    P.add("dve", lambda e: e.reduce_sum(LAMT.v(0, 1).ap, EP[0].v(0, 64).ap, mybir.AxisListType.X),
          reads=[EP[0].v(0, 64)], writes=[LAMT.v(0, 1)])
    P.add("dve", lambda e: e.reduce_sum(LAMT.v(1, 2).ap, EP[0].v(64, 128).ap, mybir.AxisListType.X),
          reads=[EP[0].v(64, 128)], writes=[LAMT.v(1, 2)])
    act(LAMT.v(2, 4), LAMT.v(0, 2), AF.Exp)
    tt(LAMT.v(4, 5), LAMT.v(3, 4), LAMT.v(2, 3), ALU.subtract)
    tsc(LAMT.v(5, 6), LAMT.v(4, 5), -LAM_INIT, None, ALU.add)
    NEGLAM = LAMT.v(5, 6)

    for i_, n_ in enumerate((D, 64, 128, 1024)):
        P.add("dve", lambda e, i_=i_, n_=n_: e.memset(TB2.v(80 + i_, 81 + i_).ap, float(n_ * EPS)), writes=[TB2.v(80 + i_, 81 + i_)])

    def rsqrt_eps(out, ss, epscol):
        act(out, ss, AF.Ln, bias=TB2.v(epscol, epscol + 1))
        act(out, out, AF.Exp, scale=-0.5)

    def tiles(ntok):
        tl = [(0, 512), (512, 1024)]
        if ntok > 1024:
            tl.append((1024, ntok))
        return tl

    def rmsnorm_to_hT(gcol0, ntok):
        for (c0, c1) in tiles(ntok):
            w = c1 - c0
            for k in range(KC):
                sq = SQ[k % 2]
                act(sq.v(0, w), X.v(k * T + c0, k * T + c1), AF.Square)
                mm(STAT.v(0, w), ONES.v(), sq.v(0, w), k == 0, k == KC - 1)
            rsqrt_eps(RS.v(c0, c1), STAT.v(0, w), 80)
            for k in range(KC):
                stt(HT.v(k * T + c0, k * T + c1), X.v(k * T + c0, k * T + c1), TB2.v(gcol0 + k, gcol0 + k + 1),
                    RS.v(c0, c1), ALU.mult, ALU.mult)

    def proj_group(slot, colfn, ntok, sm=None):
        big = nbig()
        for k in range(KC):
            lhsT = slot.v(colfn(k), colfn(k) + 128)
            for (c0, c1) in tiles(ntok):
                out = big.v(c0, c1) if c0 < 1024 else sm
                mm(out, lhsT, HT.v(k * T + c0, k * T + c1), k == 0, k == KC - 1)
        return big

    def v3of(buf, n, inner, a, b):
        t = buf.v(0, n * inner)
        return V(t.ap.rearrange("p (c t) -> p c t", t=inner)[:, :, a:b], t.key, t.lo, t.hi)

    def g16(view, n):
        return V(view.ap.rearrange("p (c t) -> p c t", t=16), view.key, view.lo, view.hi)

    def ffn(si0, ntok, gcol0):
        rmsnorm_to_hT(gcol0, ntok)
        meta = ntok > 1024
        si = si0
        for b, (f0, f1) in enumerate(FBLK):
            nfc = f1 - f0
            for s in range(f0 // 2, f1 // 2):
                slot = wget(si)
                si += 1
                for fl in range(2):
                    fc = 2 * s + fl - f0
                    bg = proj_group(slot, lambda k, fl=fl: (0 * 16 + k) * 256 + fl * 128, ntok,
                                    SMALL.v(fc * 16, fc * 16 + 16))
                    sgb = SG[(2 * s + fl) % 2]
                    act(sgb.v(0, 1024), bg.v(0, 1024), AF.Silu)
                    bu = proj_group(slot, lambda k, fl=fl: (1 * 16 + k) * 256 + fl * 128, ntok,
                                    SMALL.v(192 + fc * 16, 192 + fc * 16 + 16))
                    tt(ACT.v(fc * T, fc * T + 1024), bu.v(0, 1024), sgb.v(0, 1024), ALU.mult)
            if meta:
                act(SQ[0].v(0, nfc * 16), SMALL.v(0, nfc * 16), AF.Silu)
                tt(v3of(ACT, nfc, T, 1024, T), g16(SMALL.v(192, 192 + nfc * 16), nfc), g16(SQ[0].v(0, nfc * 16), nfc), ALU.mult)
            for dg in range(4):
                slot = wget(si)
                si += 1
                for dl in range(4):
                    dc = dg * 4 + dl
                    big = nbig()
                    for fl in range(nfc):
                        lhsT = slot.v(fl * 512 + dl * 128, fl * 512 + dl * 128 + 128)
                        for (c0, c1) in tiles(ntok):
                            out = big.v(c0, c1) if c0 < 1024 else STAT.v(dc * 16, dc * 16 + 16)
                            mm(out, lhsT, ACT.v(fl * T + c0, fl * T + c1), fl == 0, fl == nfc - 1)
                    stt(X.v(dc * T, dc * T + 1024), big.v(0, 1024), 0.5, X.v(dc * T, dc * T + 1024), ALU.mult, ALU.add)
            if meta:
                xm = v3of(X, KC, T, 1024, T)
                stt(xm, g16(STAT.v(0, 256), 16), 0.5, xm, ALU.mult, ALU.add)
        return si

    def dbg_out(view, n):
        P.add("sp", lambda e: e.dma_start(out=d_dbg.ap()[:, 0:n], in_=view.ap), reads=[view], writes=[("dbg_d", 0, 1)], chan="dbg")

    si = ffn(0, T, TG1)
    assert si == W_IN0
    def finish_dbg(view, n):
        dbg_out(view, n)
        P.add("sp", lambda e: e.nop(), reads=[("dbg_d", 0, 1)])
        with ExitStack() as st:
            P.emit(nc, st)
        return nc

    if debug is not None and debug[0] == "ffn1":
        return finish_dbg(X.v(), KC * T)

    rmsnorm_to_hT(TGM, T)
    for k in range(KC):
        dma("sp", d_xsp.ap()[:, k * NTOK:(k + 1) * NTOK], X.v(k * T, k * T + NTOK).ap,
            [X.v(k * T, k * T + NTOK)], [("xsp_d", k, k + 1)], f"xsp{k}")

    def qknorm(dst, src, n, gcol, sq):
        act(sq.v(0, n), src, AF.Square)
        big = nbig()
        for c0 in range(0, n, 512):
            c1 = min(n, c0 + 512)
            mm(big.v(c0, c1), BDG.v(), sq.v(c0, c1), True, True)
        rsqrt_eps(RS.v(0, n), big.v(0, n), 81)
        stt(dst, src, TB2.v(gcol, gcol + 1), RS.v(0, n), ALU.mult, ALU.mult)

    si = W_IN0
    KST8 = [alias(f"kst8_{i}", QT.off + i * 2048, NTOK, BF16) for i in range(8)]
    VST18 = [alias(f"vst18_{i}", U.off + i * 1024, 512, BF16) for i in range(16)]
    for sl in range(2):
        slot = wget(si)
        si += 1
        for hl in range(4):
            hh = sl * 4 + hl
            bk = proj_group(slot, lambda k, hl=hl: k * 512 + hl * 128, T, SMALL.v(hl * 16, hl * 16 + 16))
            kst = KST8[hh]
            qknorm(kst.v(0, NTOK), bk.v(0, NTOK), NTOK, 65, SQ[hh % 2])
            dma("sp", d_kin[sl].ap()[hl * 128:(hl + 1) * 128, :], kst.v(0, NTOK).ap, [kst.v(0, NTOK)], [("kin_d", hh, hh + 1)], f"kinS{sl}")
        act(SQ[0].v(0, 64), SMALL.v(0, 64), AF.Square)
        mm(STAT.v(0, 64), BDG.v(), SQ[0].v(0, 64), True, True)
        rsqrt_eps(RS.v(0, 64), STAT.v(0, 64), 81)
        for hl in range(4):
            hh = sl * 4 + hl
            stt(KMETA.v(hh * 128, hh * 128 + 16), SMALL.v(hl * 16, hl * 16 + 16), TB2.v(65, 66), RS.v(hl * 16, hl * 16 + 16), ALU.mult, ALU.mult)
        ag_after.setdefault(W_IN0 + 4 + 2 * sl, []).append(
            lambda sl=sl: P.add("pool", lambda e: e.collective_compute("AllGather", ALU.bypass, replica_groups=GROUPS,
                                                                      ins=[d_kin[sl].ap().opt()], outs=[d_kout[sl].ap().opt()]),
                                reads=[("kin_d", sl * 4, sl * 4 + 4)], writes=[("kout_d", sl, sl + 1)], chan=f"agk{sl}", inc=1))
    for sl in range(2):
        slot = wget(si)
        si += 1
        for ti in range(9):
            nt = 128 if ti < 8 else 16
            big = nbig()
            out = big.v(0, 512, 0, nt)
            for k in range(KC):
                mm(out, HT.v(k * T + ti * 128, k * T + ti * 128 + nt), slot.v(k * 512, k * 512 + 512), k == 0, k == KC - 1)
            if ti < 8:
                vst = VST18[sl * 8 + ti]
                act(vst.v(), out, AF.Copy)
                dma("sp", d_vin[sl].ap()[ti * 128:(ti + 1) * 128, :], vst.v().ap,
                    [vst.v()], [("vin_d", sl * 8 + ti, sl * 8 + ti + 1)], f"vinS{sl}")
            else:
                act(VMETA.v(sl * 512, sl * 512 + 512, 0, 16), out, AF.Copy)
        ag_after.setdefault(W_IN0 + 8 + 2 * sl, []).append(
            lambda sl=sl: P.add("pool", lambda e: e.collective_compute("AllGather", ALU.bypass, replica_groups=GROUPS,
                                                                      ins=[d_vin[sl].ap().opt()], outs=[d_vout[sl].ap().opt()]),
                                reads=[("vin_d", sl * 8, sl * 8 + 8)], writes=[("vout_d", sl, sl + 1)], chan=f"agv{sl}", inc=1))
    for sl in range(4):
        slot = wget(si)
        si += 1
        for jl in range(2):
            j = sl * 2 + jl
            ba = proj_group(slot, lambda k, jl=jl: k * 512 + (2 * jl) * 128, T, SMALL.v((2 * jl) * 16, (2 * jl) * 16 + 16))
            bgt = proj_group(slot, lambda k, jl=jl: k * 512 + (2 * jl + 1) * 128, T, SMALL.v((2 * jl + 1) * 16, (2 * jl + 1) * 16 + 16))
            sig = SIG[j % 2]
            act(sig.v(0, NTOK), bgt.v(0, NTOK), AF.Sigmoid)
            tt(U.v(j * UW + 32, j * UW + 32 + NTOK), ba.v(0, NTOK), sig.v(0, NTOK), ALU.mult)
        for jl in range(2):
            j = sl * 2 + jl
            act(SIG[0].v(0, 16), SMALL.v((2 * jl + 1) * 16, (2 * jl + 1) * 16 + 16), AF.Sigmoid)
            tt(UMETA.v(j * 16, j * 16 + 16), SMALL.v((2 * jl) * 16, (2 * jl) * 16 + 16), SIG[0].v(0, 16), ALU.mult)
    u3 = V(U.h[:, 0:8 * UW].rearrange("p (c t) -> p c t", t=UW)[:, :, 32 + NTOK - 32:32 + NTOK], "sb", U.off, U.off + 8 * UW * 4)
    dma("sp", d_hin.ap().rearrange("(j p) t -> p j t", p=128), u3.ap, [u3], [("hin_d", 0, 1)], "hin")
    ag_after.setdefault(W_IN0 + 11, []).append(
        lambda: P.add("pool", lambda e: e.collective_compute("AllGather", ALU.bypass, replica_groups=GROUPS,
                                                             ins=[d_hin.ap().opt()], outs=[d_hout.ap().opt()]),
                      reads=[("hin_d", 0, 1)], writes=[("hout_d", 0, 1)], chan="agh", inc=1))
    for sl in range(2):
        slot = wget(si)
        si += 1
        for hl in range(4):
            hh = sl * 4 + hl
            bq = proj_group(slot, lambda k, hl=hl: k * 512 + hl * 128, NTOK)
            qknorm(QT.v(hh * NTOK, (hh + 1) * NTOK), bq.v(0, NTOK), NTOK, 64, SQ[hh % 2])
    assert si == W_OUT0
    assert not ag_after, list(ag_after)
    if debug is not None and debug[0] == "win":
        return finish_dbg(QT.v(), NH * NTOK)

    dma("sp", HAL.v3(0, 4 * 8 * 32, 32).ap, d_hout.ap().rearrange("(c p) t -> p c t", p=128),
        [("hout_d", 0, 1)], [HAL.v()], "hal")
    P.add("dve", lambda e: e.memset(MH.v().ap, 0.0), writes=[MH.v()])
    mh3 = V(MH.h[:, :].rearrange("p (c t) -> p c t", t=32)[:, :, 16:32], "sb", MH.off, MH.off + 8 * 32 * 4)
    P.add("dve", lambda e: e.tensor_copy(mh3.ap, g16(UMETA.v(), 8).ap), reads=[UMETA.v()], writes=[mh3])
    uh = V(U.h[:, 0:8 * UW].rearrange("p (c t) -> p c t", t=UW)[:, :, 0:32], "sb", U.off, U.off + 8 * UW * 4)
    mhv = V(MH.h[:, :].rearrange("p (c t) -> p c t", t=32), "sb", MH.off, MH.off + 8 * 32 * 4)
    tsc(uh, mhv, TAB.v(TSEL + 4, TSEL + 5), None, ALU.mult)
    for r in range(4):
        hr = V(HAL.h[:, r * 256:(r + 1) * 256].rearrange("p (c t) -> p c t", t=32), "sb", HAL.off + r * 1024, HAL.off + (r + 1) * 1024)
        stt(uh, hr, TAB.v(TSEL + r, TSEL + r + 1), uh, ALU.mult, ALU.add)

    if debug is not None and debug[0] == "halo":
        return finish_dbg(U.v(), 8 * UW)

    def conv_chunk(j, part):
        ub = j * UW
        if part == 0:
            tsc(CACC.v(), U.v(ub + 2, ub + 2 + NTOK), TAB.v(TCW + j * 31, TCW + j * 31 + 1), TAB.v(TCB + j, TCB + j + 1), ALU.mult, ALU.add)
            taps = range(1, 16)
        else:
            taps = range(16, 30)
        for jt in taps:
            stt(CACC.v(), U.v(ub + 2 + jt, ub + 2 + jt + NTOK), TAB.v(TCW + j * 31 + jt, TCW + j * 31 + jt + 1), CACC.v(), ALU.mult, ALU.add)
        if part == 0:
            return
        y = U.v(ub + 32, ub + 32 + NTOK)
        stt(y, y, TAB.v(TCW + j * 31 + 30, TCW + j * 31 + 31), CACC.v(), ALU.mult, ALU.add)
        if j == 0:
            tt(CTMP.v(), y, y, ALU.mult)
        else:
            tt(CACC.v(), y, y, ALU.mult)
            tt(CTMP.v(), CTMP.v(), CACC.v(), ALU.add)

    def conv_finish():
        sq = SQ[0]
        P.add("dve", lambda e: e.tensor_copy(sq.v(0, NTOK).ap, CTMP.v().ap), reads=[CTMP.v()], writes=[sq.v(0, NTOK)])
        for c0 in (0, 512):
            mm(STAT.v(), ONES.v(), sq.v(c0, c0 + 512), True, True)
            rsqrt_eps(CTMP.v(c0, c0 + 512), STAT.v(), 83)
        for j in range(8):
            y = U.v(j * UW + 32, j * UW + 32 + NTOK)
            stt(CACC.v(), y, TB2.v(68 + j, 69 + j), CTMP.v(), ALU.mult, ALU.mult)
            act(CATC.v(j * NTOK, (j + 1) * NTOK), CACC.v(), AF.Silu)

    kg = [d_kout[i].ap().rearrange("(r h p) t -> h p r t", r=4, h=4, p=128) for i in range(2)]
    vg = [d_vout[i].ap().rearrange("(n p) c -> p n c", p=128) for i in range(2)]
    SPAIR = [BIG[0], BIG[1]]
    O_ = [BIG[2].v(0, 512), BIG[2].v(512, 1024)]
    SUM_ = [SMALL.v(), STAT.v()]
    PTP = [alias(f"ptp{i}", SG[0].off + i * 2048, 1024, BF16) for i in range(2)]

    def load_kv(hh):
        kb, vb = KB[hh % 2], VB[hh % 2]
        sl, hl = hh // 4, hh % 4
        dma("sp", kb.v3(0, 4 * NTOK, NTOK).ap, kg[sl][hl], [("kout_d", sl, sl + 1)], [kb.v()], f"kb{hh % 2}")
        dma("sp", vb.v3(0, 32 * 128, 128).ap, vg[sl][:, :, hl * 128:(hl + 1) * 128], [("vout_d", sl, sl + 1)], [vb.v()], f"vb{hh % 2}")

    SQF = alias("sqf", SQ[0].off, 512, F32)

    def qp_copy(hh):
        P.add("dve", lambda e: e.tensor_copy(QP.v(0, NTOK, 0, 64).ap, QT.v(hh * NTOK, (hh + 1) * NTOK, 0, 64).ap),
              reads=[QT.v(hh * NTOK, (hh + 1) * NTOK)], writes=[QP.v(0, NTOK)])
        P.add("dve", lambda e: e.tensor_copy(QP.v(NTOK, 2 * NTOK, 64, 128).ap, QT.v(hh * NTOK, (hh + 1) * NTOK, 64, 128).ap),
              reads=[QT.v(hh * NTOK, (hh + 1) * NTOK)], writes=[QP.v(NTOK, 2 * NTOK)])

    def attn_head(hh):
        kb, vb = KB[hh % 2], VB[hh % 2]
        for g in range(2):
            items = []
            tilesl = [(KMETA.v(hh * 128, hh * 128 + 128), VMETA.v(hh * 128, hh * 128 + 128), ONES16.v(), [(0, 128, 0, 512, None)])]
            for r in range(4):
                for kt in range(8):
                    kv = kb.v(r * NTOK + kt * 128, r * NTOK + kt * 128 + 128)
                    vv = vb.v((r * 8 + kt) * 128, (r * 8 + kt) * 128 + 128)
                    if kt < 4 * g:
                        rects = [(0, 128, 0, 512, TMB + 4 + r)]
                    elif kt >= 4 * g + 4:
                        rects = [(0, 128, 0, 512, TMB + r)]
                    else:
                        pp = kt - 4 * g
                        rects = []
                        if pp > 0:
                            rects.append((0, 128, 0, 128 * pp, TMB + r))
                        rects.append((0, 128, 128 * pp, 128 * pp + 64, TSA + r))
                        rects.append((0, 128, 128 * pp + 64, 512, TMB + 4 + r))
                    tilesl.append((kv, vv, ONES.v(), rects))
            n = len(tilesl)

            def issue_s(t):
                kv = tilesl[t][0]
                for c in range(2):
                    mm(SPAIR[t % 2].v(c * 512, c * 512 + 512), kv, QP.v(c * NTOK + g * 512, c * NTOK + g * 512 + 512), True, True)

            def pair3(buf, c0, c1):
                t_ = buf.v(0, 1024)
                return V(t_.ap.rearrange("p (c t) -> p c t", t=512)[:, :, c0:c1], t_.key, t_.lo, t_.hi)

            issue_s(0)
            issue_s(1)
            for t in range(n):
                kv, vv, ov, rects = tilesl[t]
                sp_, pt = SPAIR[t % 2], PTP[t % 2]
                for ri, (p0, p1, c0, c1, bcol) in enumerate(rects):
                    bias = None if bcol is None else TAB.v(bcol, bcol + 1)
                    act(pair3(pt, c0, c1), pair3(sp_, c0, c1), AF.Exp, bias=bias, nosame=(ri > 0))
                for c in range(2):
                    mm(O_[c], vv, pt.v(c * 512, c * 512 + 512), t == 0, t == n - 1)
                if t + 2 < n:
                    issue_s(t + 2)
                for c in range(2):
                    mm(SUM_[c], ov, pt.v(c * 512, c * 512 + 512), t == 0, t == n - 1)
            if g == 1 and hh + 1 < NH:
                qp_copy(hh + 1)
            n_ = hh * 2 + g
            stmp = [RS.v(0, 512), SQF.v()]
            for c in range(2):
                P.add("dve", lambda e, c=c: e.tensor_copy(EP[2 + c].v().ap, O_[c].ap), reads=[O_[c]], writes=[EP[2 + c].v()])
            for c in range(2):
                P.add("dve", lambda e, c=c: e.tensor_copy(stmp[c].ap, SUM_[c].ap), reads=[SUM_[c]], writes=[stmp[c]])
            if pend[0] is not None:
                attn_finalize(pend[0])
            for c in range(2):
                P.add("dve", lambda e, c=c: e.reciprocal(stmp[c].ap, stmp[c].ap), reads=[stmp[c]], writes=[stmp[c]])
                tt(EP[2 + c].v(), EP[2 + c].v(), stmp[c], ALU.mult)
            ob = OB[n_ % 2]
            stt(ob.v(), EP[3].v(), NEGLAM, EP[2].v(), ALU.mult, ALU.add)
            tt(SQ[1].v((n_ % 2) * 512, (n_ % 2) * 512 + 512), ob.v(), ob.v(), ALU.mult)
            pend[0] = (hh, g, n_)
            if g == 0:
                conv_chunk(hh, 1)

    OB = [EP[4], alloc("ob1", 512, F32)]
    pend = [None]

    def attn_finalize(p):
        hh, g, n_ = p
        mm(O_[0], ONES.v(), SQ[1].v((n_ % 2) * 512, (n_ % 2) * 512 + 512), True, True)
        act(EP[0].v(), O_[0], AF.Ln, bias=TB2.v(82, 83))
        act(EP[0].v(), EP[0].v(), AF.Exp, scale=-0.5)
        stt(CATA.v(hh * NTOK + g * 512, hh * NTOK + g * 512 + 512), OB[n_ % 2].v(), TB2.v(66, 67), EP[0].v(), ALU.mult, ALU.mult)

    load_kv(0)
    qp_copy(0)
    for hh in range(NH):
        if hh + 1 < NH:
            load_kv(hh + 1)
        conv_chunk(hh, 0)
        attn_head(hh)
        if debug is not None and debug[0] == "attn0":
            return finish_dbg(CATA.v(), NH * NTOK)
    attn_finalize(pend[0])
    conv_finish()

    WTMP = alias("wtmp", HT.off, 4 * NTOK, F32)

    def reload_x(k):
        dma("sp", X.v(k * T, k * T + NTOK).ap, d_xsp.ap()[:, k * NTOK:(k + 1) * NTOK],
            [("xsp_d", k, k + 1)], [X.v(k * T, k * T + NTOK)], f"xrl{k}")

    for k in range(12):
        reload_x(k)
    si = W_OUT0
    for dg in range(4):
        slot = wget(si)
        si += 1
        for dl in range(4):
            dc = dg * 4 + dl
            big = nbig()
            for c in range(16):
                lhsT = slot.v(c * 512 + dl * 128, c * 512 + dl * 128 + 128)
                src = CATA if c < 8 else CATC
                cc = c % 8
                for (c0, c1) in tiles(NTOK):
                    mm(big.v(c0, c1), lhsT, src.v(cc * NTOK + c0, cc * NTOK + c1), c == 0, c == 15)
            if dc < 12:
                tt(X.v(dc * T, dc * T + NTOK), big.v(0, NTOK), X.v(dc * T, dc * T + NTOK), ALU.add)
            else:
                act(WTMP.v((dc - 12) * NTOK, (dc - 11) * NTOK), big.v(0, NTOK), AF.Copy)
    for k in range(12, 16):
        reload_x(k)
        tt(X.v(k * T, k * T + NTOK), WTMP.v((k - 12) * NTOK, (k - 11) * NTOK), X.v(k * T, k * T + NTOK), ALU.add)
    assert si == F2_0

    si = ffn(F2_0, NTOK, TG2)
    assert si == len(stream)
    for (c0, c1) in tiles(NTOK):
        w = c1 - c0
        for k in range(KC):
            sq = SQ[k % 2]
            act(sq.v(0, w), X.v(k * T + c0, k * T + c1), AF.Square)
            mm(STAT.v(0, w), ONES.v(), sq.v(0, w), k == 0, k == KC - 1)
        rsqrt_eps(RS.v(c0, c1), STAT.v(0, w), 80)
        for k in range(KC):
            o = EP[k % 4]
            stt(o.v(0, w), X.v(k * T + c0, k * T + c1), TB2.v(TGF + k, TGF + k + 1), RS.v(c0, c1), ALU.mult, ALU.mult)
            dma("sp", d_out.ap()[:, k * NTOK + c0:k * NTOK + c1], o.v(0, w).ap, [o.v(0, w)], [("out_d", 0, 1)], f"out{k % 4}")
    P.add("sp", lambda e: e.nop(), reads=[("out_d", 0, 1)])
    with ExitStack() as st:
        P.emit(nc, st)
    return nc


def _prep_gu(wg, wu):
    out = np.empty((22, 128, 2, 16, 256), np.float32)
    for which, w in enumerate((wg, wu)):
        out[:, :, which] = w.reshape(16, 128, 22, 256).transpose(2, 1, 0, 3)
    return out.reshape(22, 128, WSLOT)


def _prep_d(wd):
    out = np.zeros((16, 128, 12, 512), np.float32)
    for b, (f0, f1) in enumerate(FBLK):
        blk = wd[f0 * 128:f1 * 128].reshape(f1 - f0, 128, 4, 512)
        out[b * 4:(b + 1) * 4, :, :f1 - f0] = blk.transpose(2, 1, 0, 3)
    return out.reshape(16, 128, 12 * 512)


def _prep_win(w_in):
    order = []
    order += [8 + h for h in range(8)]
    order += [16 + h for h in range(8)]
    for i in range(4):
        order += [24 + 2 * i, 32 + 2 * i, 24 + 2 * i + 1, 32 + 2 * i + 1]
    order += [h for h in range(8)]
    cols = np.concatenate([np.arange(c * 128, (c + 1) * 128) for c in order])
    w = w_in[:, cols]
    return np.ascontiguousarray(w.reshape(16, 128, 10, 512).transpose(2, 1, 0, 3)).reshape(10, 128, WSLOT)


def _prep_wout(w_out):
    return np.ascontiguousarray(w_out.reshape(16, 128, 4, 512).transpose(2, 1, 0, 3)).reshape(4, 128, WSLOT)


def _col(v):
    return np.ascontiguousarray(v.reshape(-1, 128).T)


def make_in_maps(inp):
    f32 = np.float32
    x = np.asarray(inp["x"], f32)
    meta = np.asarray(inp["meta_tokens"], f32)
    shared = {
        "f1gu": _prep_gu(np.asarray(inp["ffn1_w_gate"], f32)[0], np.asarray(inp["ffn1_w_up"], f32)[0]),
        "f1d": _prep_d(np.asarray(inp["ffn1_w_down"], f32)[0]),
        "f2gu": _prep_gu(np.asarray(inp["ffn2_w_gate"], f32)[0], np.asarray(inp["ffn2_w_up"], f32)[0]),
        "f2d": _prep_d(np.asarray(inp["ffn2_w_down"], f32)[0]),
        "win": _prep_win(np.asarray(inp["w_in"], f32)[0]),
        "wout": _prep_wout(np.asarray(inp["w_out"], f32)[0]),
    }
    tab0 = np.zeros((128, TABN), f32)
    tab0[:, TG1:TG1 + 16] = _col(np.asarray(inp["ffn1_norm_g"], f32)[0])
    tab0[:, TGM:TGM + 16] = _col(np.asarray(inp["mix_norm_g"], f32)[0])
    tab0[:, TG2:TG2 + 16] = _col(np.asarray(inp["ffn2_norm_g"], f32)[0])
    tab0[:, TGF:TGF + 16] = _col(np.asarray(inp["final_norm_g"], f32)[0])
    tab0[:, TGQ] = np.tile(np.asarray(inp["q_norm_g"], f32)[0], 2)
    tab0[:, TGK] = np.tile(np.asarray(inp["k_norm_g"], f32)[0], 2)
    tab0[:, TGS] = np.asarray(inp["attn_subln_g"], f32)[0]
    tab0[:, TCB:TCB + 8] = _col(np.asarray(inp["conv_b"], f32)[0])
    tab0[:, TGC:TGC + 8] = _col(np.asarray(inp["conv_norm_g"], f32)[0])
    cw = np.asarray(inp["conv_w"], f32)[0][:, 0, :]
    tab0[:, TCW:TCW + 248] = cw.reshape(31, 8, 128).transpose(2, 1, 0).reshape(128, 248)
    lam = np.concatenate([np.asarray(inp[k], f32)[0] for k in ("lambda_q1", "lambda_k1", "lambda_q2", "lambda_k2")])
    tab0[:, TLAM:TLAM + 256] = lam[None, :]
    in_maps = []
    for c in range(8):
        b, j = c // 4, c % 4
        tok = np.concatenate([x[b, j * NTOK:(j + 1) * NTOK], meta], axis=0)
        xT = np.ascontiguousarray(tok.reshape(T, KC, 128).transpose(2, 1, 0)).reshape(128, KC * T)
        tab = tab0.copy()
        sel = np.zeros(5, f32)
        if j == 0:
            sel[4] = 1.0
        else:
            sel[j - 1] = 1.0
        tab[:, TSEL:TSEL + 5] = sel[None, :]
        mb = np.zeros(8, f32)
        for r in range(4):
            mb[r] = 0.0 if r < j else NEG
            mb[4 + r] = 0.0 if r <= j else NEG
        tab[:, TMB:TMB + 8] = mb[None, :]
        for r in range(4):
            tab[:64, TSA + r] = mb[4 + r]
            tab[64:, TSA + r] = mb[r]
        m = dict(shared)
        m["xT"] = xT
        m["tab"] = tab
        in_maps.append(m)
    return in_maps


DEBUG = None


def kernel(**inputs):
    in_maps = make_in_maps(inputs)
    if DEBUG is not None:
        nc = build_program(DEBUG)
        res = run_bass_kernel_spmd(nc, in_maps, core_ids=list(range(8)))
        return [r["dbg"] for r in res.results]
    nc = build_program()
    res = run_bass_kernel_spmd(nc, in_maps, core_ids=list(range(8)))
    out = np.empty((2, 4 * NTOK, D), np.float32)
    for c in range(8):
        b, j = c // 4, c % 4
        oT = np.asarray(res.results[c]["outT"], np.float32).reshape(128, KC, NTOK)
        out[b, j * NTOK:(j + 1) * NTOK] = oT.transpose(2, 1, 0).reshape(NTOK, D)
    return out
```

```python
import numpy as np
from contextlib import ExitStack

import concourse.bass as bass
import concourse.mybir as mybir
from concourse.bass_utils import run_bass_kernel_spmd

F32 = mybir.dt.float32
BF16 = mybir.dt.bfloat16
ALU = mybir.AluOpType
AF = mybir.ActivationFunctionType

ENGS = ("pe", "act", "dve", "pool", "sp")
SEM_LIM = 3000


class Op:
    __slots__ = ("eng", "fn", "deps", "chan", "chan_val", "needed", "tick", "dma_deps", "inc")


class V:
    __slots__ = ("ap", "key", "lo", "hi")

    def __init__(self, ap, key, lo, hi):
        self.ap, self.key, self.lo, self.hi = ap, key, lo, hi

    def reg(self):
        return (self.key, self.lo, self.hi)


class Buf:
    def __init__(self, handle, name, n, dtype, esz):
        self.h, self.name, self.n, self.dtype, self.esz = handle, name, n, dtype, esz

    def v(self, a=0, b=None, p0=None, p1=None):
        if b is None:
            b = self.n
        if p0 is None:
            ap = self.h[:, a:b]
        else:
            ap = self.h[p0:p1, a:b]
        return V(ap, self.name, a * self.esz, b * self.esz)

    def v3(self, a, b, inner, p0=None, p1=None):
        t = self.v(a, b, p0, p1)
        return V(t.ap.rearrange("p (c t) -> p c t", t=inner), t.key, t.lo, t.hi)


class Prog:
    def __init__(self):
        self.ops = {e: [] for e in ENGS}
        self.hist = {}
        self.chan_cnt = {}
        self.nops = 0

    def add(self, eng, fn, reads=(), writes=(), chan=None, inc=16, nosame=False):
        op = Op()
        op.eng, op.fn, op.chan, op.needed, op.tick = eng, fn, chan, False, 0
        op.deps = []
        op.dma_deps = {}
        op.chan_val = 0
        self._nosame = nosame
        reads = [r.reg() if isinstance(r, V) else r for r in reads]
        writes = [w.reg() if isinstance(w, V) else w for w in writes]
        ps_r = [(k, lo // 2048 * 2048, (hi + 2047) // 2048 * 2048) for (k, lo, hi) in reads if k.startswith("ps_")]
        reads = [r for r in reads if not r[0].startswith("ps_")]
        writes = [(k, lo // 2048 * 2048, (hi + 2047) // 2048 * 2048) if k.startswith("ps_") else (k, lo, hi)
                  for (k, lo, hi) in writes] + ps_r
        for (key, lo, hi) in reads:
            for ent in self.hist.get(key, ()):
                if ent[3] and ent[0] < hi and lo < ent[1]:
                    self._dep(op, ent[2], True)
        for (key, lo, hi) in writes:
            lst = self.hist.setdefault(key, [])
            keep = []
            for ent in lst:
                if ent[0] < hi and lo < ent[1]:
                    self._dep(op, ent[2], ent[3])
                    if lo <= ent[0] and ent[1] <= hi:
                        continue
                keep.append(ent)
            keep.append([lo, hi, op, True])
            self.hist[key] = keep
        for (key, lo, hi) in reads:
            lst = self.hist.setdefault(key, [])
            if chan is None:
                for ent in lst:
                    if (not ent[3]) and ent[0] == lo and ent[1] == hi and ent[2].eng == eng \
                            and ent[2].chan is None:
                        ent[2] = op
                        break
                else:
                    lst.append([lo, hi, op, False])
            else:
                lst.append([lo, hi, op, False])
        if chan is not None:
            self.chan_cnt[chan] = self.chan_cnt.get(chan, 0) + inc
            op.chan_val = self.chan_cnt[chan]
        op.inc = inc
        self.ops[eng].append(op)
        self.nops += 1
        return op

    def _dep(self, op, d, strong):
        if d is op:
            return
        if d.chan is not None:
            cur = self.chan_cnt[d.chan]
            if op.dma_deps.get(d.chan, 0) < cur:
                op.dma_deps[d.chan] = cur
            return
        if d.eng == op.eng:
            if op.chan is None and (d.eng == "pe" or self._nosame):
                return
        d.needed = True
        op.deps.append(d)

    def emit(self, nc, stack):
        eng_sems = {}
        for e in ENGS:
            n_needed = sum(1 for o in self.ops[e] if o.needed)
            k = n_needed // SEM_LIM + 1
            eng_sems[e] = [stack.enter_context(nc.semaphore(f"s_{e}{i}")) for i in range(k)]
            t = 0
            for o in self.ops[e]:
                if o.needed:
                    t += 1
                    o.tick = t
        chan_sems = {c: stack.enter_context(nc.semaphore(f"c_{c}")) for c in self.chan_cnt}
        prog = self

        def replay(ename, eng):
            waited = {}
            for o in prog.ops[ename]:
                wl = {}
                for d in o.deps:
                    si = (d.tick - 1) // SEM_LIM
                    val = d.tick - si * SEM_LIM
                    kk = ("e", d.eng, si)
                    if wl.get(kk, 0) < val:
                        wl[kk] = val
                for c, val in o.dma_deps.items():
                    kk = ("c", c)
                    if wl.get(kk, 0) < val:
                        wl[kk] = val
                for kk, val in wl.items():
                    if waited.get(kk, 0) >= val:
                        continue
                    waited[kk] = val
                    sem = eng_sems[kk[1]][kk[2]] if kk[0] == "e" else chan_sems[kk[1]]
                    eng.wait_ge(sem, val)
                inst = o.fn(eng)
                if o.chan is not None:
                    inst.then_inc(chan_sems[o.chan], o.inc)
                elif o.needed:
                    si = (o.tick - 1) // SEM_LIM
                    inst.then_inc(eng_sems[ename][si], 1)

        block = stack.enter_context(nc.Block())

        @block.tensor
        def _(eng):
            replay("pe", eng)

        @block.scalar
        def _(eng):
            replay("act", eng)

        @block.vector
        def _(eng):
            replay("dve", eng)

        @block.gpsimd
        def _(eng):
            replay("pool", eng)

        @block.sync
        def _(eng):
            replay("sp", eng)


D = 2048
KC = 16
DFF = 5632
FCN = 44
NTOK = 1024
NMETA = 16
T = NTOK + NMETA
NH = 8
EPS = 1e-6
LAM_INIT = 0.8 - 0.6 * float(np.exp(-0.3))
FBLK = [(0, 12), (12, 24), (24, 34), (34, 44)]
WSLOT = 8192
NSLOT = 3
NEG = -30000.0

TG1, TGM, TG2, TGF = 0, 16, 32, 48
TGQ, TGK, TGS = 64, 65, 66
TCB, TGC = 68, 76
TCW = 84
TSEL = TCW + 248
TMB = TSEL + 5
TSA = TMB + 8
TLAM = TSA + 4
TABN = TLAM + 256


class SBuf(Buf):
    def __init__(self, nc, name, off, n, dtype, base):
        esz = 4 if dtype == F32 else 2
        h = nc.alloc_sbuf_tensor_at(name, [128, n], dtype, offset=base + off)
        Buf.__init__(self, h, "sb", n, dtype, esz)
        self.off = off

    def v(self, a=0, b=None, p0=None, p1=None):
        if b is None:
            b = self.n
        ap = self.h[:, a:b] if p0 is None else self.h[p0:p1, a:b]
        return V(ap, "sb", self.off + a * self.esz, self.off + b * self.esz)


class PBuf(Buf):
    def __init__(self, nc, name, n):
        h = nc.alloc_psum_tensor(name, [128, n], F32)
        Buf.__init__(self, h, "ps_" + name, n, F32, 4)


def build_program(debug=None):
    nc = bass.Bass("TRN2", target_bir_lowering=False)
    P = Prog()
    base = (nc.sbuf_base + 63) // 64 * 64
    cap = nc.sbuf_top - base

    d_xT = nc.dram_tensor("xT", [128, KC * T], F32, kind="ExternalInput")
    d_tab = nc.dram_tensor("tab", [128, TABN], F32, kind="ExternalInput")
    d_f1gu = nc.dram_tensor("f1gu", [22, 128, WSLOT], F32, kind="ExternalInput")
    d_f1d = nc.dram_tensor("f1d", [16, 128, 12 * 512], F32, kind="ExternalInput")
    d_f2gu = nc.dram_tensor("f2gu", [22, 128, WSLOT], F32, kind="ExternalInput")
    d_f2d = nc.dram_tensor("f2d", [16, 128, 12 * 512], F32, kind="ExternalInput")
    d_win = nc.dram_tensor("win", [10, 128, WSLOT], F32, kind="ExternalInput")
    d_wout = nc.dram_tensor("wout", [4, 128, WSLOT], F32, kind="ExternalInput")
    d_out = nc.dram_tensor("outT", [128, KC * NTOK], F32, kind="ExternalOutput")
    d_xsp = nc.dram_tensor("xspill", [128, KC * NTOK], F32, kind="Internal")
    d_kin = [nc.dram_tensor(f"k_gin{i}", [4 * 128, NTOK], BF16, kind="Internal") for i in range(2)]
    d_vin = [nc.dram_tensor(f"v_gin{i}", [NTOK, 512], BF16, kind="Internal") for i in range(2)]
    d_hin = nc.dram_tensor("h_gin", [NH * 128, 32], F32, kind="Internal")
    d_kout = [nc.dram_tensor(f"k_gout{i}", [4 * 4 * 128, NTOK], BF16, kind="Internal") for i in range(2)]
    d_vout = [nc.dram_tensor(f"v_gout{i}", [4 * NTOK, 512], BF16, kind="Internal") for i in range(2)]
    d_hout = nc.dram_tensor("h_gout", [4 * NH * 128, 32], F32, kind="Internal")
    d_dbg = None
    if debug is not None:
        d_dbg = nc.dram_tensor("dbg", [128, debug[1]], F32 if debug[2] == "f32" else BF16, kind="ExternalOutput")
    GROUPS = [[0, 1, 2, 3], [4, 5, 6, 7]]

    off = [0]

    def alloc(name, n, dtype):
        esz = 4 if dtype == F32 else 2
        b = SBuf(nc, name, off[0], n, dtype, base)
        off[0] += (n * esz + 63) // 64 * 64
        assert off[0] <= cap, (name, off[0], cap)
        return b

    def alias(name, o, n, dtype):
        return SBuf(nc, name, o, n, dtype, base)

    X = alloc("x", KC * T, F32)
    HT = alloc("hT", KC * T, BF16)
    ACT = alloc("act", 12 * T + 320, BF16)
    WS = [alloc(f"ws{i}", WSLOT, BF16) for i in range(NSLOT)]
    SG = [alloc(f"sg{i}", T, BF16) for i in range(2)]
    RS = alloc("rs", T, F32)
    SQ = [alloc(f"sq{i}", T, BF16) for i in range(2)]
    TAB = alloc("tab", TABN, F32)
    TB2 = alloc("tab2", 96, F32)
    ONES = alloc("ones", 128, BF16)
    BDG = alloc("bdg", 128, BF16)
    KMETA = alloc("kmeta", NH * 128, BF16)
    ONES16 = alloc("ones16", 128, BF16)
    VMETA = alloc("vmeta", 1024, BF16)
    UMETA = alloc("umeta", 8 * 16, F32)
    LAMT = alloc("lamt", 8, F32)
    MISC0 = off[0]
    UW = 1056
    U = alias("u", X.off, 8 * UW, F32)
    QT = alias("qT", X.off + 8 * UW * 4, NH * NTOK, BF16)
    CATC = alias("catc", X.off + 8 * UW * 4 + NH * NTOK * 2, 8 * NTOK, BF16)
    assert 8 * UW * 4 + NH * NTOK * 2 + 8 * NTOK * 2 <= KC * T * 4
    CATA = alias("cata", ACT.off, 8 * NTOK, BF16)
    ATMP = ACT.off + 8 * NTOK * 2
    KB = [alias(f"kb{i}", HT.off + i * 16384, 4 * NTOK, BF16) for i in range(2)]
    VB = [alias(f"vb{i}", HT.off + i * 16384 + 8192, 32 * 128, BF16) for i in range(2)]
    o_ = ACT.off
    KST = [alias(f"kst{i}", o_ + i * 2112, T, BF16) for i in range(2)]
    VST = [alias(f"vst{i}", o_ + 4224 + i * 1024, 512, BF16) for i in range(2)]
    SIG = [alias(f"sig{i}", o_ + 6272 + i * 4160, T, F32) for i in range(2)]
    assert 6272 + 2 * 4160 <= 16384
    EP = [alias(f"ep{i}", ATMP + i * 2048, 512, F32) for i in range(4)]
    assert 4 * 2048 <= 9216
    EP.append(alloc("ep4", 512, F32))
    PT = [alias(f"pt{i}", SG[0].off + i * 1024, 512, BF16) for i in range(4)]
    assert SG[1].off + 2112 - SG[0].off >= 4096
    CACC = alloc("cacc", NTOK, F32)
    CTMP = alloc("ctmp", NTOK, F32)
    HAL = alias("hal", SG[0].off, 4 * 8 * 32, F32)
    MH = alias("mh", ATMP + 8192, 8 * 32, F32)
    QP = alloc("qp", 2 * NTOK, BF16)
    print("SBUF used", off[0], "of", cap)

    BIG = [PBuf(nc, f"big{i}", 1024) for i in range(3)]
    SMALL = PBuf(nc, "small", 512)
    STAT = PBuf(nc, "stat", 512)
    cnt = {"big": 0, "small": 0, "slot": 0}

    def nbig():
        b = BIG[cnt["big"] % 3]
        cnt["big"] += 1
        return b

    def mm(out, lhsT, rhs, start, stop):
        P.add("pe", lambda e: e.matmul(out.ap, lhsT.ap, rhs.ap, start=start, stop=stop),
              reads=[lhsT, rhs], writes=[out])

    def act(out, in_, func, bias=None, scale=1.0, nosame=False):
        rd = [in_] + ([bias] if isinstance(bias, V) else [])
        if bias is None:
            P.add("act", lambda e: e.activation(out.ap, in_.ap, func, scale=scale), reads=rd, writes=[out], nosame=nosame)
        else:
            b = bias.ap if isinstance(bias, V) else bias
            P.add("act", lambda e: e.activation(out.ap, in_.ap, func, bias=b, scale=scale), reads=rd, writes=[out], nosame=nosame)

    def _s(x):
        return x.ap if isinstance(x, V) else x

    def tsc(out, in0, s1, s2, op0, op1=None, eng="dve"):
        rd = [in0] + [s for s in (s1, s2) if isinstance(s, V)]
        if op1 is None:
            P.add(eng, lambda e: e.tensor_scalar(out.ap, in0.ap, _s(s1), None, op0), reads=rd, writes=[out])
        else:
            P.add(eng, lambda e: e.tensor_scalar(out.ap, in0.ap, _s(s1), _s(s2), op0, op1), reads=rd, writes=[out])

    def stt(out, in0, sc, in1, op0, op1, eng="dve"):
        rd = [in0, in1] + ([sc] if isinstance(sc, V) else [])
        P.add(eng, lambda e: e.scalar_tensor_tensor(out.ap, in0.ap, _s(sc), in1.ap, op0, op1), reads=rd, writes=[out])

    def tt(out, in0, in1, op, eng="dve"):
        P.add(eng, lambda e: e.tensor_tensor(out.ap, in0.ap, in1.ap, op), reads=[in0, in1], writes=[out])

    def dma(eng, out, in_, reads, writes, chan):
        P.add(eng, lambda e: e.dma_start(out=out, in_=in_), reads=reads, writes=writes, chan=chan)

    stream = []
    for b, (c0, c1) in enumerate(FBLK):
        for s in range(c0 // 2, c1 // 2):
            stream.append((d_f1gu, s, WSLOT))
        for dg in range(4):
            stream.append((d_f1d, b * 4 + dg, (c1 - c0) * 512))
    W_IN0 = len(stream)
    for s in range(10):
        stream.append((d_win, s, WSLOT))
    W_OUT0 = len(stream)
    for s in range(4):
        stream.append((d_wout, s, WSLOT))
    F2_0 = len(stream)
    for b, (c0, c1) in enumerate(FBLK):
        for s in range(c0 // 2, c1 // 2):
            stream.append((d_f2gu, s, WSLOT))
        for dg in range(4):
            stream.append((d_f2d, b * 4 + dg, (c1 - c0) * 512))
    issued = [0]
    ag_after = {}

    WS3 = alias("ws3", X.off + 8 * 1056 * 4 + NH * NTOK * 2, WSLOT, BF16)
    ring4 = [WS[2], WS[0], WS[1], WS3]
    slot_of = []
    for j in range(len(stream)):
        if W_IN0 <= j < W_IN0 + 10:
            r_ = (j - W_IN0) % 4
            slot_of.append((ring4[r_], "ws3" if r_ == 3 else f"ws{(2, 0, 1)[r_]}"))
        elif j < W_IN0:
            slot_of.append((WS[j % 3], f"ws{j % 3}"))
        else:
            jj = j - (W_IN0 + 10)
            slot_of.append((WS[(jj + 1) % 3], f"ws{(jj + 1) % 3}"))
    assert W_IN0 % 3 == 2, W_IN0

    def wget(i):
        la = 3 if W_IN0 <= i < W_IN0 + 8 else 2
        while issued[0] < min(len(stream), i + la + 1):
            j = issued[0]
            h, idx, ne = stream[j]
            slot, chn = slot_of[j]
            prev = [p for p in range(j) if slot_of[p][0] is slot]
            assert not prev or prev[-1] < i, (i, j, prev[-1])
            dma("pool", slot.v(0, ne).ap, h.ap()[idx][:, 0:ne], [(h.name, idx, idx + 1)], [slot.v(0, ne)], chn)
            issued[0] += 1
            for th in ag_after.pop(j, []):
                th()
        return slot_of[i][0]

    dma("sp", TAB.v().ap, d_tab.ap(), [("tab_d", 0, 1)], [TAB.v()], "tab")
    for ti_, (c0_, c1_) in enumerate(((0, 512), (512, 1024), (1024, T))):
        for k in range(KC):
            dma("sp", X.v(k * T + c0_, k * T + c1_).ap, d_xT.ap()[:, k * T + c0_:k * T + c1_], [("xT_d", 0, 1)],
                [X.v(k * T + c0_, k * T + c1_)], f"xin{ti_}")
    P.add("dve", lambda e: e.memset(ONES.v().ap, 1.0), writes=[ONES.v()])
    P.add("dve", lambda e: e.memset(ONES16.v().ap, 0.0), writes=[ONES16.v()])
    P.add("dve", lambda e: e.memset(ONES16.v(0, 128, 0, 16).ap, 1.0), writes=[ONES16.v()])
    P.add("dve", lambda e: e.memset(QP.v().ap, 0.0), writes=[QP.v()])
    P.add("dve", lambda e: e.memset(KMETA.v().ap, 0.0), writes=[KMETA.v()])
    P.add("dve", lambda e: e.memset(VMETA.v().ap, 0.0), writes=[VMETA.v()])
    P.add("dve", lambda e: e.memset(BDG.v().ap, 0.0), writes=[BDG.v()])
    P.add("dve", lambda e: e.memset(BDG.v(0, 64, 0, 64).ap, 1.0), writes=[BDG.v()])
    P.add("dve", lambda e: e.memset(BDG.v(64, 128, 64, 128).ap, 1.0), writes=[BDG.v()])
    tsc(TB2.v(0, 64), TAB.v(0, 64), float(np.sqrt(D)), None, ALU.mult)
    tsc(TB2.v(64, 65), TAB.v(TGQ, TGQ + 1), 1.0, None, ALU.mult)
    tsc(TB2.v(65, 66), TAB.v(TGK, TGK + 1), 8.0, None, ALU.mult)
    tsc(TB2.v(66, 67), TAB.v(TGS, TGS + 1), float(np.sqrt(128.0) * (1.0 - LAM_INIT)), None, ALU.mult)
    tsc(TB2.v(68, 76), TAB.v(TGC, TGC + 8), 32.0, None, ALU.mult)
    tt(EP[0].v(0, 64), TAB.v(TLAM, TLAM + 64), TAB.v(TLAM + 64, TLAM + 128), ALU.mult)
    tt(EP[0].v(64, 128), TAB.v(TLAM + 128, TLAM + 192), TAB.v(TLAM + 192, TLAM + 256), ALU.mult)
    P.add("dve", lambda e: e.reduce_sum(LAMT.v(0, 1).ap, EP[0].v(0, 64).ap, mybir.AxisListType.X),
          reads=[EP[0].v(0, 64)], writes=[LAMT.v(0, 1)])
    P.add("dve", lambda e: e.reduce_sum(LAMT.v(1, 2).ap, EP[0].v(64, 128).ap, mybir.AxisListType.X),
          reads=[EP[0].v(64, 128)], writes=[LAMT.v(1, 2)])
    act(LAMT.v(2, 4), LAMT.v(0, 2), AF.Exp)
    tt(LAMT.v(4, 5), LAMT.v(3, 4), LAMT.v(2, 3), ALU.subtract)
    tsc(LAMT.v(5, 6), LAMT.v(4, 5), -LAM_INIT, None, ALU.add)
    NEGLAM = LAMT.v(5, 6)

    for i_, n_ in enumerate((D, 64, 128, 1024)):
        P.add("dve", lambda e, i_=i_, n_=n_: e.memset(TB2.v(80 + i_, 81 + i_).ap, float(n_ * EPS)), writes=[TB2.v(80 + i_, 81 + i_)])

    def rsqrt_eps(out, ss, epscol):
        act(out, ss, AF.Ln, bias=TB2.v(epscol, epscol + 1))
        act(out, out, AF.Exp, scale=-0.5)

    def tiles(ntok):
        tl = [(0, 512), (512, 1024)]
        if ntok > 1024:
            tl.append((1024, ntok))
        return tl

    def rmsnorm_to_hT(gcol0, ntok):
        for (c0, c1) in tiles(ntok):
            w = c1 - c0
            for k in range(KC):
                sq = SQ[k % 2]
                act(sq.v(0, w), X.v(k * T + c0, k * T + c1), AF.Square)
                mm(STAT.v(0, w), ONES.v(), sq.v(0, w), k == 0, k == KC - 1)
            rsqrt_eps(RS.v(c0, c1), STAT.v(0, w), 80)
            for k in range(KC):
                stt(HT.v(k * T + c0, k * T + c1), X.v(k * T + c0, k * T + c1), TB2.v(gcol0 + k, gcol0 + k + 1),
                    RS.v(c0, c1), ALU.mult, ALU.mult)

    def proj_group(slot, colfn, ntok, sm=None):
        big = nbig()
        for k in range(KC):
            lhsT = slot.v(colfn(k), colfn(k) + 128)
            for (c0, c1) in tiles(ntok):
                out = big.v(c0, c1) if c0 < 1024 else sm
                mm(out, lhsT, HT.v(k * T + c0, k * T + c1), k == 0, k == KC - 1)
        return big

    def v3of(buf, n, inner, a, b):
        t = buf.v(0, n * inner)
        return V(t.ap.rearrange("p (c t) -> p c t", t=inner)[:, :, a:b], t.key, t.lo, t.hi)

    def g16(view, n):
        return V(view.ap.rearrange("p (c t) -> p c t", t=16), view.key, view.lo, view.hi)

    def ffn(si0, ntok, gcol0):
        rmsnorm_to_hT(gcol0, ntok)
        meta = ntok > 1024
        si = si0
        for b, (f0, f1) in enumerate(FBLK):
            nfc = f1 - f0
            for s in range(f0 // 2, f1 // 2):
                slot = wget(si)
                si += 1
                for fl in range(2):
                    fc = 2 * s + fl - f0
                    bg = proj_group(slot, lambda k, fl=fl: (0 * 16 + k) * 256 + fl * 128, ntok,
                                    SMALL.v(fc * 16, fc * 16 + 16))
                    sgb = SG[(2 * s + fl) % 2]
                    act(sgb.v(0, 1024), bg.v(0, 1024), AF.Silu)
                    bu = proj_group(slot, lambda k, fl=fl: (1 * 16 + k) * 256 + fl * 128, ntok,
                                    SMALL.v(192 + fc * 16, 192 + fc * 16 + 16))
                    tt(ACT.v(fc * T, fc * T + 1024), bu.v(0, 1024), sgb.v(0, 1024), ALU.mult)
            if meta:
                act(SQ[0].v(0, nfc * 16), SMALL.v(0, nfc * 16), AF.Silu)
                tt(v3of(ACT, nfc, T, 1024, T), g16(SMALL.v(192, 192 + nfc * 16), nfc), g16(SQ[0].v(0, nfc * 16), nfc), ALU.mult)
            for dg in range(4):
                slot = wget(si)
                si += 1
                for dl in range(4):
                    dc = dg * 4 + dl
                    big = nbig()
                    for fl in range(nfc):
                        lhsT = slot.v(fl * 512 + dl * 128, fl * 512 + dl * 128 + 128)
                        for (c0, c1) in tiles(ntok):
                            out = big.v(c0, c1) if c0 < 1024 else STAT.v(dc * 16, dc * 16 + 16)
                            mm(out, lhsT, ACT.v(fl * T + c0, fl * T + c1), fl == 0, fl == nfc - 1)
                    stt(X.v(dc * T, dc * T + 1024), big.v(0, 1024), 0.5, X.v(dc * T, dc * T + 1024), ALU.mult, ALU.add)
            if meta:
                xm = v3of(X, KC, T, 1024, T)
                stt(xm, g16(STAT.v(0, 256), 16), 0.5, xm, ALU.mult, ALU.add)
        return si

    def dbg_out(view, n):
        P.add("sp", lambda e: e.dma_start(out=d_dbg.ap()[:, 0:n], in_=view.ap), reads=[view], writes=[("dbg_d", 0, 1)], chan="dbg")

    si = ffn(0, T, TG1)
    assert si == W_IN0
    def finish_dbg(view, n):
        dbg_out(view, n)
        P.add("sp", lambda e: e.nop(), reads=[("dbg_d", 0, 1)])
        with ExitStack() as st:
            P.emit(nc, st)
        return nc

    if debug is not None and debug[0] == "ffn1":
        return finish_dbg(X.v(), KC * T)

    rmsnorm_to_hT(TGM, T)
    for k in range(KC):
        dma("sp", d_xsp.ap()[:, k * NTOK:(k + 1) * NTOK], X.v(k * T, k * T + NTOK).ap,
            [X.v(k * T, k * T + NTOK)], [("xsp_d", k, k + 1)], "xsp")

    def qknorm(dst, src, n, gcol, sq):
        act(sq.v(0, n), src, AF.Square)
        big = nbig()
        for c0 in range(0, n, 512):
            c1 = min(n, c0 + 512)
            mm(big.v(c0, c1), BDG.v(), sq.v(c0, c1), True, True)
        rsqrt_eps(RS.v(0, n), big.v(0, n), 81)
        stt(dst, src, TB2.v(gcol, gcol + 1), RS.v(0, n), ALU.mult, ALU.mult)

    si = W_IN0
    KST8 = [alias(f"kst8_{i}", QT.off + i * 2048, NTOK, BF16) for i in range(8)]
    VST18 = [alias(f"vst18_{i}", U.off + i * 1024, 512, BF16) for i in range(16)]
    for sl in range(2):
        slot = wget(si)
        si += 1
        for hl in range(4):
            hh = sl * 4 + hl
            bk = proj_group(slot, lambda k, hl=hl: k * 512 + hl * 128, T, SMALL.v(hl * 16, hl * 16 + 16))
            kst = KST8[hh]
            qknorm(kst.v(0, NTOK), bk.v(0, NTOK), NTOK, 65, SQ[hh % 2])
            dma("sp", d_kin[sl].ap()[hl * 128:(hl + 1) * 128, :], kst.v(0, NTOK).ap, [kst.v(0, NTOK)], [("kin_d", hh, hh + 1)], f"kinS{sl}")
        act(SQ[0].v(0, 64), SMALL.v(0, 64), AF.Square)
        mm(STAT.v(0, 64), BDG.v(), SQ[0].v(0, 64), True, True)
        rsqrt_eps(RS.v(0, 64), STAT.v(0, 64), 81)
        for hl in range(4):
            hh = sl * 4 + hl
            stt(KMETA.v(hh * 128, hh * 128 + 16), SMALL.v(hl * 16, hl * 16 + 16), TB2.v(65, 66), RS.v(hl * 16, hl * 16 + 16), ALU.mult, ALU.mult)
        ag_after.setdefault(W_IN0 + 4 + 2 * sl, []).append(
            lambda sl=sl: P.add("pool", lambda e: e.collective_compute("AllGather", ALU.bypass, replica_groups=GROUPS,
                                                                      ins=[d_kin[sl].ap().opt()], outs=[d_kout[sl].ap().opt()]),
                                reads=[("kin_d", sl * 4, sl * 4 + 4)], writes=[("kout_d", sl, sl + 1)], chan=f"agk{sl}", inc=1))
    for sl in range(2):
        slot = wget(si)
        si += 1
        for ti in range(9):
            nt = 128 if ti < 8 else 16
            big = nbig()
            out = big.v(0, 512, 0, nt)
            for k in range(KC):
                mm(out, HT.v(k * T + ti * 128, k * T + ti * 128 + nt), slot.v(k * 512, k * 512 + 512), k == 0, k == KC - 1)
            if ti < 8:
                vst = VST18[sl * 8 + ti]
                act(vst.v(), out, AF.Copy)
                dma("sp", d_vin[sl].ap()[ti * 128:(ti + 1) * 128, :], vst.v().ap,
                    [vst.v()], [("vin_d", sl * 8 + ti, sl * 8 + ti + 1)], f"vinS{sl}")
            else:
                act(VMETA.v(sl * 512, sl * 512 + 512, 0, 16), out, AF.Copy)
        ag_after.setdefault(W_IN0 + 8 + 2 * sl, []).append(
            lambda sl=sl: P.add("pool", lambda e: e.collective_compute("AllGather", ALU.bypass, replica_groups=GROUPS,
                                                                      ins=[d_vin[sl].ap().opt()], outs=[d_vout[sl].ap().opt()]),
                                reads=[("vin_d", sl * 8, sl * 8 + 8)], writes=[("vout_d", sl, sl + 1)], chan=f"agv{sl}", inc=1))
    for sl in range(4):
        slot = wget(si)
        si += 1
        for jl in range(2):
            j = sl * 2 + jl
            ba = proj_group(slot, lambda k, jl=jl: k * 512 + (2 * jl) * 128, T, SMALL.v((2 * jl) * 16, (2 * jl) * 16 + 16))
            bgt = proj_group(slot, lambda k, jl=jl: k * 512 + (2 * jl + 1) * 128, T, SMALL.v((2 * jl + 1) * 16, (2 * jl + 1) * 16 + 16))
            sig = SIG[j % 2]
            act(sig.v(0, NTOK), bgt.v(0, NTOK), AF.Sigmoid)
            tt(U.v(j * UW + 32, j * UW + 32 + NTOK), ba.v(0, NTOK), sig.v(0, NTOK), ALU.mult)
        for jl in range(2):
            j = sl * 2 + jl
            act(SIG[0].v(0, 16), SMALL.v((2 * jl + 1) * 16, (2 * jl + 1) * 16 + 16), AF.Sigmoid)
            tt(UMETA.v(j * 16, j * 16 + 16), SMALL.v((2 * jl) * 16, (2 * jl) * 16 + 16), SIG[0].v(0, 16), ALU.mult)
    u3 = V(U.h[:, 0:8 * UW].rearrange("p (c t) -> p c t", t=UW)[:, :, 32 + NTOK - 32:32 + NTOK], "sb", U.off, U.off + 8 * UW * 4)
    dma("sp", d_hin.ap().rearrange("(j p) t -> p j t", p=128), u3.ap, [u3], [("hin_d", 0, 1)], "hin")
    ag_after.setdefault(W_IN0 + 11, []).append(
        lambda: P.add("pool", lambda e: e.collective_compute("AllGather", ALU.bypass, replica_groups=GROUPS,
                                                             ins=[d_hin.ap().opt()], outs=[d_hout.ap().opt()]),
                      reads=[("hin_d", 0, 1)], writes=[("hout_d", 0, 1)], chan="agh", inc=1))
    for sl in range(2):
        slot = wget(si)
        si += 1
        for hl in range(4):
            hh = sl * 4 + hl
            bq = proj_group(slot, lambda k, hl=hl: k * 512 + hl * 128, NTOK)
            qknorm(QT.v(hh * NTOK, (hh + 1) * NTOK), bq.v(0, NTOK), NTOK, 64, SQ[hh % 2])
    assert si == W_OUT0
    assert not ag_after, list(ag_after)
    if debug is not None and debug[0] == "win":
        return finish_dbg(QT.v(), NH * NTOK)

    dma("sp", HAL.v3(0, 4 * 8 * 32, 32).ap, d_hout.ap().rearrange("(c p) t -> p c t", p=128),
        [("hout_d", 0, 1)], [HAL.v()], "hal")
    P.add("dve", lambda e: e.memset(MH.v().ap, 0.0), writes=[MH.v()])
    mh3 = V(MH.h[:, :].rearrange("p (c t) -> p c t", t=32)[:, :, 16:32], "sb", MH.off, MH.off + 8 * 32 * 4)
    P.add("dve", lambda e: e.tensor_copy(mh3.ap, g16(UMETA.v(), 8).ap), reads=[UMETA.v()], writes=[mh3])
    uh = V(U.h[:, 0:8 * UW].rearrange("p (c t) -> p c t", t=UW)[:, :, 0:32], "sb", U.off, U.off + 8 * UW * 4)
    mhv = V(MH.h[:, :].rearrange("p (c t) -> p c t", t=32), "sb", MH.off, MH.off + 8 * 32 * 4)
    tsc(uh, mhv, TAB.v(TSEL + 4, TSEL + 5), None, ALU.mult)
    for r in range(4):
        hr = V(HAL.h[:, r * 256:(r + 1) * 256].rearrange("p (c t) -> p c t", t=32), "sb", HAL.off + r * 1024, HAL.off + (r + 1) * 1024)
        stt(uh, hr, TAB.v(TSEL + r, TSEL + r + 1), uh, ALU.mult, ALU.add)

    if debug is not None and debug[0] == "halo":
        return finish_dbg(U.v(), 8 * UW)

    def conv_chunk(j, part):
        ub = j * UW
        if part == 0:
            tsc(CACC.v(), U.v(ub + 2, ub + 2 + NTOK), TAB.v(TCW + j * 31, TCW + j * 31 + 1), TAB.v(TCB + j, TCB + j + 1), ALU.mult, ALU.add)
            taps = range(1, 16)
        else:
            taps = range(16, 30)
        for jt in taps:
            stt(CACC.v(), U.v(ub + 2 + jt, ub + 2 + jt + NTOK), TAB.v(TCW + j * 31 + jt, TCW + j * 31 + jt + 1), CACC.v(), ALU.mult, ALU.add)
        if part == 0:
            return
        y = U.v(ub + 32, ub + 32 + NTOK)
        stt(y, y, TAB.v(TCW + j * 31 + 30, TCW + j * 31 + 31), CACC.v(), ALU.mult, ALU.add)
        if j == 0:
            tt(CTMP.v(), y, y, ALU.mult)
        else:
            tt(CACC.v(), y, y, ALU.mult)
            tt(CTMP.v(), CTMP.v(), CACC.v(), ALU.add)

    def conv_finish():
        sq = SQ[0]
        P.add("dve", lambda e: e.tensor_copy(sq.v(0, NTOK).ap, CTMP.v().ap), reads=[CTMP.v()], writes=[sq.v(0, NTOK)])
        for c0 in (0, 512):
            mm(STAT.v(), ONES.v(), sq.v(c0, c0 + 512), True, True)
            rsqrt_eps(CTMP.v(c0, c0 + 512), STAT.v(), 83)
        for j in range(8):
            y = U.v(j * UW + 32, j * UW + 32 + NTOK)
            stt(CACC.v(), y, TB2.v(68 + j, 69 + j), CTMP.v(), ALU.mult, ALU.mult)
            act(CATC.v(j * NTOK, (j + 1) * NTOK), CACC.v(), AF.Silu)

    kg = [d_kout[i].ap().rearrange("(r h p) t -> h p r t", r=4, h=4, p=128) for i in range(2)]
    vg = [d_vout[i].ap().rearrange("(n p) c -> p n c", p=128) for i in range(2)]
    SPAIR = [BIG[0], BIG[1]]
    O_ = [BIG[2].v(0, 512), BIG[2].v(512, 1024)]
    SUM_ = [SMALL.v(), STAT.v()]
    PTP = [alias(f"ptp{i}", SG[0].off + i * 2048, 1024, BF16) for i in range(2)]

    def load_kv(hh):
        kb, vb = KB[hh % 2], VB[hh % 2]
        sl, hl = hh // 4, hh % 4
        dma("sp", kb.v3(0, 4 * NTOK, NTOK).ap, kg[sl][hl], [("kout_d", sl, sl + 1)], [kb.v()], f"kb{hh % 2}")
        dma("sp", vb.v3(0, 32 * 128, 128).ap, vg[sl][:, :, hl * 128:(hl + 1) * 128], [("vout_d", sl, sl + 1)], [vb.v()], f"vb{hh % 2}")

    SQF = alias("sqf", SQ[0].off, 512, F32)

    def qp_copy(hh):
        P.add("dve", lambda e: e.tensor_copy(QP.v(0, NTOK, 0, 64).ap, QT.v(hh * NTOK, (hh + 1) * NTOK, 0, 64).ap),
              reads=[QT.v(hh * NTOK, (hh + 1) * NTOK)], writes=[QP.v(0, NTOK)])
        P.add("dve", lambda e: e.tensor_copy(QP.v(NTOK, 2 * NTOK, 64, 128).ap, QT.v(hh * NTOK, (hh + 1) * NTOK, 64, 128).ap),
              reads=[QT.v(hh * NTOK, (hh + 1) * NTOK)], writes=[QP.v(NTOK, 2 * NTOK)])

    def attn_head(hh):
        kb, vb = KB[hh % 2], VB[hh % 2]
        for g in range(2):
            items = []
            tilesl = [(KMETA.v(hh * 128, hh * 128 + 128), VMETA.v(hh * 128, hh * 128 + 128), ONES16.v(), [(0, 128, 0, 512, None)])]
            for r in range(4):
                for kt in range(8):
                    kv = kb.v(r * NTOK + kt * 128, r * NTOK + kt * 128 + 128)
                    vv = vb.v((r * 8 + kt) * 128, (r * 8 + kt) * 128 + 128)
                    if kt < 4 * g:
                        rects = [(0, 128, 0, 512, TMB + 4 + r)]
                    elif kt >= 4 * g + 4:
                        rects = [(0, 128, 0, 512, TMB + r)]
                    else:
                        pp = kt - 4 * g
                        rects = []
                        if pp > 0:
                            rects.append((0, 128, 0, 128 * pp, TMB + r))
                        rects.append((0, 128, 128 * pp, 128 * pp + 64, TSA + r))
                        rects.append((0, 128, 128 * pp + 64, 512, TMB + 4 + r))
                    tilesl.append((kv, vv, ONES.v(), rects))
            n = len(tilesl)

            def issue_s(t):
                kv = tilesl[t][0]
                for c in range(2):
                    mm(SPAIR[t % 2].v(c * 512, c * 512 + 512), kv, QP.v(c * NTOK + g * 512, c * NTOK + g * 512 + 512), True, True)

            def pair3(buf, c0, c1):
                t_ = buf.v(0, 1024)
                return V(t_.ap.rearrange("p (c t) -> p c t", t=512)[:, :, c0:c1], t_.key, t_.lo, t_.hi)

            issue_s(0)
            issue_s(1)
            for t in range(n):
                kv, vv, ov, rects = tilesl[t]
                sp_, pt = SPAIR[t % 2], PTP[t % 2]
                for ri, (p0, p1, c0, c1, bcol) in enumerate(rects):
                    bias = None if bcol is None else TAB.v(bcol, bcol + 1)
                    act(pair3(pt, c0, c1), pair3(sp_, c0, c1), AF.Exp, bias=bias, nosame=(ri > 0))
                for c in range(2):
                    mm(O_[c], vv, pt.v(c * 512, c * 512 + 512), t == 0, t == n - 1)
                if t + 2 < n:
                    issue_s(t + 2)
                for c in range(2):
                    mm(SUM_[c], ov, pt.v(c * 512, c * 512 + 512), t == 0, t == n - 1)
            if g == 1 and hh + 1 < NH:
                qp_copy(hh + 1)
            n_ = hh * 2 + g
            stmp = [RS.v(0, 512), SQF.v()]
            for c in range(2):
                P.add("dve", lambda e, c=c: e.tensor_copy(EP[2 + c].v().ap, O_[c].ap), reads=[O_[c]], writes=[EP[2 + c].v()])
            for c in range(2):
                P.add("dve", lambda e, c=c: e.tensor_copy(stmp[c].ap, SUM_[c].ap), reads=[SUM_[c]], writes=[stmp[c]])
            if pend[0] is not None:
                attn_finalize(pend[0])
            for c in range(2):
                P.add("dve", lambda e, c=c: e.reciprocal(stmp[c].ap, stmp[c].ap), reads=[stmp[c]], writes=[stmp[c]])
                tt(EP[2 + c].v(), EP[2 + c].v(), stmp[c], ALU.mult)
            ob = OB[n_ % 2]
            stt(ob.v(), EP[3].v(), NEGLAM, EP[2].v(), ALU.mult, ALU.add)
            tt(SQ[1].v((n_ % 2) * 512, (n_ % 2) * 512 + 512), ob.v(), ob.v(), ALU.mult)
            pend[0] = (hh, g, n_)
            if g == 0:
                conv_chunk(hh, 1)

    OB = [EP[4], alloc("ob1", 512, F32)]
    pend = [None]

    def attn_finalize(p):
        hh, g, n_ = p
        mm(O_[0], ONES.v(), SQ[1].v((n_ % 2) * 512, (n_ % 2) * 512 + 512), True, True)
        act(EP[0].v(), O_[0], AF.Ln, bias=TB2.v(82, 83))
        act(EP[0].v(), EP[0].v(), AF.Exp, scale=-0.5)
        stt(CATA.v(hh * NTOK + g * 512, hh * NTOK + g * 512 + 512), OB[n_ % 2].v(), TB2.v(66, 67), EP[0].v(), ALU.mult, ALU.mult)

    load_kv(0)
    qp_copy(0)
    for hh in range(NH):
        if hh + 1 < NH:
            load_kv(hh + 1)
        conv_chunk(hh, 0)
        attn_head(hh)
        if debug is not None and debug[0] == "attn0":
            return finish_dbg(CATA.v(), NH * NTOK)
    attn_finalize(pend[0])
    conv_finish()

    WTMP = alias("wtmp", HT.off, 4 * NTOK, F32)

    def reload_x(k):
        dma("sp", X.v(k * T, k * T + NTOK).ap, d_xsp.ap()[:, k * NTOK:(k + 1) * NTOK],
            [("xsp_d", k, k + 1)], [X.v(k * T, k * T + NTOK)], f"xrl{k}")

    for k in range(12):
        reload_x(k)
    si = W_OUT0
    for dg in range(4):
        slot = wget(si)
        si += 1
        for dl in range(4):
            dc = dg * 4 + dl
            big = nbig()
            for c in range(16):
                lhsT = slot.v(c * 512 + dl * 128, c * 512 + dl * 128 + 128)
                src = CATA if c < 8 else CATC
                cc = c % 8
                for (c0, c1) in tiles(NTOK):
                    mm(big.v(c0, c1), lhsT, src.v(cc * NTOK + c0, cc * NTOK + c1), c == 0, c == 15)
            if dc < 12:
                tt(X.v(dc * T, dc * T + NTOK), big.v(0, NTOK), X.v(dc * T, dc * T + NTOK), ALU.add)
            else:
                act(WTMP.v((dc - 12) * NTOK, (dc - 11) * NTOK), big.v(0, NTOK), AF.Copy)
    for k in range(12, 16):
        reload_x(k)
        tt(X.v(k * T, k * T + NTOK), WTMP.v((k - 12) * NTOK, (k - 11) * NTOK), X.v(k * T, k * T + NTOK), ALU.add)
    assert si == F2_0

    si = ffn(F2_0, NTOK, TG2)
    assert si == len(stream)
    for (c0, c1) in tiles(NTOK):
        w = c1 - c0
        for k in range(KC):
            sq = SQ[k % 2]
            act(sq.v(0, w), X.v(k * T + c0, k * T + c1), AF.Square)
            mm(STAT.v(0, w), ONES.v(), sq.v(0, w), k == 0, k == KC - 1)
        rsqrt_eps(RS.v(c0, c1), STAT.v(0, w), 80)
        for k in range(KC):
            o = EP[k % 4]
            stt(o.v(0, w), X.v(k * T + c0, k * T + c1), TB2.v(TGF + k, TGF + k + 1), RS.v(c0, c1), ALU.mult, ALU.mult)
            dma("sp", d_out.ap()[:, k * NTOK + c0:k * NTOK + c1], o.v(0, w).ap, [o.v(0, w)], [("out_d", 0, 1)], f"out{k % 4}")
    P.add("sp", lambda e: e.nop(), reads=[("out_d", 0, 1)])
    with ExitStack() as st:
        P.emit(nc, st)
    return nc


def _prep_gu(wg, wu):
    out = np.empty((22, 128, 2, 16, 256), np.float32)
    for which, w in enumerate((wg, wu)):
        out[:, :, which] = w.reshape(16, 128, 22, 256).transpose(2, 1, 0, 3)
    return out.reshape(22, 128, WSLOT)


def _prep_d(wd):
    out = np.zeros((16, 128, 12, 512), np.float32)
    for b, (f0, f1) in enumerate(FBLK):
        blk = wd[f0 * 128:f1 * 128].reshape(f1 - f0, 128, 4, 512)
        out[b * 4:(b + 1) * 4, :, :f1 - f0] = blk.transpose(2, 1, 0, 3)
    return out.reshape(16, 128, 12 * 512)


def _prep_win(w_in):
    order = []
    order += [8 + h for h in range(8)]
    order += [16 + h for h in range(8)]
    for i in range(4):
        order += [24 + 2 * i, 32 + 2 * i, 24 + 2 * i + 1, 32 + 2 * i + 1]
    order += [h for h in range(8)]
    cols = np.concatenate([np.arange(c * 128, (c + 1) * 128) for c in order])
    w = w_in[:, cols]
    return np.ascontiguousarray(w.reshape(16, 128, 10, 512).transpose(2, 1, 0, 3)).reshape(10, 128, WSLOT)


def _prep_wout(w_out):
    return np.ascontiguousarray(w_out.reshape(16, 128, 4, 512).transpose(2, 1, 0, 3)).reshape(4, 128, WSLOT)


def _col(v):
    return np.ascontiguousarray(v.reshape(-1, 128).T)


def make_in_maps(inp):
    f32 = np.float32
    x = np.asarray(inp["x"], f32)
    meta = np.asarray(inp["meta_tokens"], f32)
    shared = {
        "f1gu": _prep_gu(np.asarray(inp["ffn1_w_gate"], f32)[0], np.asarray(inp["ffn1_w_up"], f32)[0]),
        "f1d": _prep_d(np.asarray(inp["ffn1_w_down"], f32)[0]),
        "f2gu": _prep_gu(np.asarray(inp["ffn2_w_gate"], f32)[0], np.asarray(inp["ffn2_w_up"], f32)[0]),
        "f2d": _prep_d(np.asarray(inp["ffn2_w_down"], f32)[0]),
        "win": _prep_win(np.asarray(inp["w_in"], f32)[0]),
        "wout": _prep_wout(np.asarray(inp["w_out"], f32)[0]),
    }
    tab0 = np.zeros((128, TABN), f32)
    tab0[:, TG1:TG1 + 16] = _col(np.asarray(inp["ffn1_norm_g"], f32)[0])
    tab0[:, TGM:TGM + 16] = _col(np.asarray(inp["mix_norm_g"], f32)[0])
    tab0[:, TG2:TG2 + 16] = _col(np.asarray(inp["ffn2_norm_g"], f32)[0])
    tab0[:, TGF:TGF + 16] = _col(np.asarray(inp["final_norm_g"], f32)[0])
    tab0[:, TGQ] = np.tile(np.asarray(inp["q_norm_g"], f32)[0], 2)
    tab0[:, TGK] = np.tile(np.asarray(inp["k_norm_g"], f32)[0], 2)
    tab0[:, TGS] = np.asarray(inp["attn_subln_g"], f32)[0]
    tab0[:, TCB:TCB + 8] = _col(np.asarray(inp["conv_b"], f32)[0])
    tab0[:, TGC:TGC + 8] = _col(np.asarray(inp["conv_norm_g"], f32)[0])
    cw = np.asarray(inp["conv_w"], f32)[0][:, 0, :]
    tab0[:, TCW:TCW + 248] = cw.reshape(31, 8, 128).transpose(2, 1, 0).reshape(128, 248)
    lam = np.concatenate([np.asarray(inp[k], f32)[0] for k in ("lambda_q1", "lambda_k1", "lambda_q2", "lambda_k2")])
    tab0[:, TLAM:TLAM + 256] = lam[None, :]
    in_maps = []
    for c in range(8):
        b, j = c // 4, c % 4
        tok = np.concatenate([x[b, j * NTOK:(j + 1) * NTOK], meta], axis=0)
        xT = np.ascontiguousarray(tok.reshape(T, KC, 128).transpose(2, 1, 0)).reshape(128, KC * T)
        tab = tab0.copy()
        sel = np.zeros(5, f32)
        if j == 0:
            sel[4] = 1.0
        else:
            sel[j - 1] = 1.0
        tab[:, TSEL:TSEL + 5] = sel[None, :]
        mb = np.zeros(8, f32)
        for r in range(4):
            mb[r] = 0.0 if r < j else NEG
            mb[4 + r] = 0.0 if r <= j else NEG
        tab[:, TMB:TMB + 8] = mb[None, :]
        for r in range(4):
            tab[:64, TSA + r] = mb[4 + r]
            tab[64:, TSA + r] = mb[r]
        m = dict(shared)
        m["xT"] = xT
        m["tab"] = tab
        in_maps.append(m)
    return in_maps


DEBUG = None


def kernel(**inputs):
    in_maps = make_in_maps(inputs)
    if DEBUG is not None:
        nc = build_program(DEBUG)
        res = run_bass_kernel_spmd(nc, in_maps, core_ids=list(range(8)))
        return [r["dbg"] for r in res.results]
    nc = build_program()
    res = run_bass_kernel_spmd(nc, in_maps, core_ids=list(range(8)))
    out = np.empty((2, 4 * NTOK, D), np.float32)
    for c in range(8):
        b, j = c // 4, c % 4
        oT = np.asarray(res.results[c]["outT"], np.float32).reshape(128, KC, NTOK)
        out[b, j * NTOK:(j + 1) * NTOK] = oT.transpose(2, 1, 0).reshape(NTOK, D)
    return out
```

```python
import numpy as np
from contextlib import ExitStack

import concourse.bass as bass
import concourse.mybir as mybir
from concourse.bass_utils import run_bass_kernel_spmd

F32 = mybir.dt.float32
BF16 = mybir.dt.bfloat16
ALU = mybir.AluOpType
AF = mybir.ActivationFunctionType

ENGS = ("pe", "act", "dve", "pool", "sp")
SEM_LIM = 3000


class Op:
    __slots__ = ("eng", "fn", "deps", "chan", "chan_val", "needed", "tick", "dma_deps", "inc")


class V:
    __slots__ = ("ap", "key", "lo", "hi")

    def __init__(self, ap, key, lo, hi):
        self.ap, self.key, self.lo, self.hi = ap, key, lo, hi

    def reg(self):
        return (self.key, self.lo, self.hi)


class Buf:
    def __init__(self, handle, name, n, dtype, esz):
        self.h, self.name, self.n, self.dtype, self.esz = handle, name, n, dtype, esz

    def v(self, a=0, b=None, p0=None, p1=None):
        if b is None:
            b = self.n
        if p0 is None:
            ap = self.h[:, a:b]
        else:
            ap = self.h[p0:p1, a:b]
        return V(ap, self.name, a * self.esz, b * self.esz)

    def v3(self, a, b, inner, p0=None, p1=None):
        t = self.v(a, b, p0, p1)
        return V(t.ap.rearrange("p (c t) -> p c t", t=inner), t.key, t.lo, t.hi)


class Prog:
    def __init__(self):
        self.ops = {e: [] for e in ENGS}
        self.hist = {}
        self.chan_cnt = {}
        self.nops = 0

    def add(self, eng, fn, reads=(), writes=(), chan=None, inc=16, nosame=False):
        op = Op()
        op.eng, op.fn, op.chan, op.needed, op.tick = eng, fn, chan, False, 0
        op.deps = []
        op.dma_deps = {}
        op.chan_val = 0
        self._nosame = nosame
        reads = [r.reg() if isinstance(r, V) else r for r in reads]
        writes = [w.reg() if isinstance(w, V) else w for w in writes]
        ps_r = [(k, lo // 2048 * 2048, (hi + 2047) // 2048 * 2048) for (k, lo, hi) in reads if k.startswith("ps_")]
        reads = [r for r in reads if not r[0].startswith("ps_")]
        writes = [(k, lo // 2048 * 2048, (hi + 2047) // 2048 * 2048) if k.startswith("ps_") else (k, lo, hi)
                  for (k, lo, hi) in writes] + ps_r
        for (key, lo, hi) in reads:
            for ent in self.hist.get(key, ()):
                if ent[3] and ent[0] < hi and lo < ent[1]:
                    self._dep(op, ent[2], True)
        for (key, lo, hi) in writes:
            lst = self.hist.setdefault(key, [])
            keep = []
            for ent in lst:
                if ent[0] < hi and lo < ent[1]:
                    self._dep(op, ent[2], ent[3])
                    if lo <= ent[0] and ent[1] <= hi:
                        continue
                keep.append(ent)
            keep.append([lo, hi, op, True])
            self.hist[key] = keep
        for (key, lo, hi) in reads:
            lst = self.hist.setdefault(key, [])
            if chan is None:
                for ent in lst:
                    if (not ent[3]) and ent[0] == lo and ent[1] == hi and ent[2].eng == eng \
                            and ent[2].chan is None:
                        ent[2] = op
                        break
                else:
                    lst.append([lo, hi, op, False])
            else:
                lst.append([lo, hi, op, False])
        if chan is not None:
            self.chan_cnt[chan] = self.chan_cnt.get(chan, 0) + inc
            op.chan_val = self.chan_cnt[chan]
        op.inc = inc
        self.ops[eng].append(op)
        self.nops += 1
        return op

    def _dep(self, op, d, strong):
        if d is op:
            return
        if d.chan is not None:
            cur = self.chan_cnt[d.chan]
            if op.dma_deps.get(d.chan, 0) < cur:
                op.dma_deps[d.chan] = cur
            return
        if d.eng == op.eng:
            if op.chan is None and (d.eng == "pe" or self._nosame):
                return
        d.needed = True
        op.deps.append(d)

    def emit(self, nc, stack):
        eng_sems = {}
        for e in ENGS:
            n_needed = sum(1 for o in self.ops[e] if o.needed)
            k = n_needed // SEM_LIM + 1
            eng_sems[e] = [stack.enter_context(nc.semaphore(f"s_{e}{i}")) for i in range(k)]
            t = 0
            for o in self.ops[e]:
                if o.needed:
                    t += 1
                    o.tick = t
        chan_sems = {c: stack.enter_context(nc.semaphore(f"c_{c}")) for c in self.chan_cnt}
        prog = self

        def replay(ename, eng):
            waited = {}
            for o in prog.ops[ename]:
                wl = {}
                for d in o.deps:
                    si = (d.tick - 1) // SEM_LIM
                    val = d.tick - si * SEM_LIM
                    kk = ("e", d.eng, si)
                    if wl.get(kk, 0) < val:
                        wl[kk] = val
                for c, val in o.dma_deps.items():
                    kk = ("c", c)
                    if wl.get(kk, 0) < val:
                        wl[kk] = val
                for kk, val in wl.items():
                    if waited.get(kk, 0) >= val:
                        continue
                    waited[kk] = val
                    sem = eng_sems[kk[1]][kk[2]] if kk[0] == "e" else chan_sems[kk[1]]
                    eng.wait_ge(sem, val)
                inst = o.fn(eng)
                if o.chan is not None:
                    inst.then_inc(chan_sems[o.chan], o.inc)
                elif o.needed:
                    si = (o.tick - 1) // SEM_LIM
                    inst.then_inc(eng_sems[ename][si], 1)

        block = stack.enter_context(nc.Block())

        @block.tensor
        def _(eng):
            replay("pe", eng)

        @block.scalar
        def _(eng):
            replay("act", eng)

        @block.vector
        def _(eng):
            replay("dve", eng)

        @block.gpsimd
        def _(eng):
            replay("pool", eng)

        @block.sync
        def _(eng):
            replay("sp", eng)


D = 2048
KC = 16
DFF = 5632
FCN = 44
NTOK = 1024
NMETA = 16
T = NTOK + NMETA
NH = 8
EPS = 1e-6
LAM_INIT = 0.8 - 0.6 * float(np.exp(-0.3))
FBLK = [(0, 12), (12, 24), (24, 34), (34, 44)]
WSLOT = 8192
NSLOT = 3
NEG = -30000.0

TG1, TGM, TG2, TGF = 0, 16, 32, 48
TGQ, TGK, TGS = 64, 65, 66
TCB, TGC = 68, 76
TCW = 84
TSEL = TCW + 248
TMB = TSEL + 5
TSA = TMB + 8
TLAM = TSA + 4
TABN = TLAM + 256


class SBuf(Buf):
    def __init__(self, nc, name, off, n, dtype, base):
        esz = 4 if dtype == F32 else 2
        h = nc.alloc_sbuf_tensor_at(name, [128, n], dtype, offset=base + off)
        Buf.__init__(self, h, "sb", n, dtype, esz)
        self.off = off

    def v(self, a=0, b=None, p0=None, p1=None):
        if b is None:
            b = self.n
        ap = self.h[:, a:b] if p0 is None else self.h[p0:p1, a:b]
        return V(ap, "sb", self.off + a * self.esz, self.off + b * self.esz)


class PBuf(Buf):
    def __init__(self, nc, name, n):
        h = nc.alloc_psum_tensor(name, [128, n], F32)
        Buf.__init__(self, h, "ps_" + name, n, F32, 4)


def build_program(debug=None):
    nc = bass.Bass("TRN2", target_bir_lowering=False)
    P = Prog()
    base = (nc.sbuf_base + 63) // 64 * 64
    cap = nc.sbuf_top - base

    d_xT = nc.dram_tensor("xT", [128, KC * T], F32, kind="ExternalInput")
    d_tab = nc.dram_tensor("tab", [128, TABN], F32, kind="ExternalInput")
    d_f1gu = nc.dram_tensor("f1gu", [22, 128, WSLOT], F32, kind="ExternalInput")
    d_f1d = nc.dram_tensor("f1d", [16, 128, 12 * 512], F32, kind="ExternalInput")
    d_f2gu = nc.dram_tensor("f2gu", [22, 128, WSLOT], F32, kind="ExternalInput")
    d_f2d = nc.dram_tensor("f2d", [16, 128, 12 * 512], F32, kind="ExternalInput")
    d_win = nc.dram_tensor("win", [10, 128, WSLOT], F32, kind="ExternalInput")
    d_wout = nc.dram_tensor("wout", [4, 128, WSLOT], F32, kind="ExternalInput")
    d_out = nc.dram_tensor("outT", [128, KC * NTOK], F32, kind="ExternalOutput")
    d_xsp = nc.dram_tensor("xspill", [128, KC * NTOK], F32, kind="Internal")
    d_kin = [nc.dram_tensor(f"k_gin{i}", [4 * 128, NTOK], BF16, kind="Internal") for i in range(2)]
    d_vin = [nc.dram_tensor(f"v_gin{i}", [NTOK, 512], BF16, kind="Internal") for i in range(2)]
    d_hin = nc.dram_tensor("h_gin", [NH * 128, 32], F32, kind="Internal")
    d_kout = [nc.dram_tensor(f"k_gout{i}", [4 * 4 * 128, NTOK], BF16, kind="Internal") for i in range(2)]
    d_vout = [nc.dram_tensor(f"v_gout{i}", [4 * NTOK, 512], BF16, kind="Internal") for i in range(2)]
    d_hout = nc.dram_tensor("h_gout", [4 * NH * 128, 32], F32, kind="Internal")
    d_dbg = None
    if debug is not None:
        d_dbg = nc.dram_tensor("dbg", [128, debug[1]], F32 if debug[2] == "f32" else BF16, kind="ExternalOutput")
    GROUPS = [[0, 1, 2, 3], [4, 5, 6, 7]]

    off = [0]

    def alloc(name, n, dtype):
        esz = 4 if dtype == F32 else 2
        b = SBuf(nc, name, off[0], n, dtype, base)
        off[0] += (n * esz + 63) // 64 * 64
        assert off[0] <= cap, (name, off[0], cap)
        return b

    def alias(name, o, n, dtype):
        return SBuf(nc, name, o, n, dtype, base)

    X = alloc("x", KC * T, F32)
    HT = alloc("hT", KC * T, BF16)
    ACT = alloc("act", 12 * T + 320, BF16)
    WS = [alloc(f"ws{i}", WSLOT, BF16) for i in range(NSLOT)]
    SG = [alloc(f"sg{i}", T, BF16) for i in range(2)]
    RS = alloc("rs", T, F32)
    SQ = [alloc(f"sq{i}", T, BF16) for i in range(2)]
    TAB = alloc("tab", TABN, F32)
    TB2 = alloc("tab2", 96, F32)
    ONES = alloc("ones", 128, BF16)
    BDG = alloc("bdg", 128, BF16)
    KMETA = alloc("kmeta", NH * 128, BF16)
    ONES16 = alloc("ones16", 128, BF16)
    VMETA = alloc("vmeta", 1024, BF16)
    UMETA = alloc("umeta", 8 * 16, F32)
    LAMT = alloc("lamt", 8, F32)
    MISC0 = off[0]
    UW = 1056
    U = alias("u", X.off, 8 * UW, F32)
    QT = alias("qT", X.off + 8 * UW * 4, NH * NTOK, BF16)
    CATC = alias("catc", X.off + 8 * UW * 4 + NH * NTOK * 2, 8 * NTOK, BF16)
    assert 8 * UW * 4 + NH * NTOK * 2 + 8 * NTOK * 2 <= KC * T * 4
    CATA = alias("cata", ACT.off, 8 * NTOK, BF16)
    ATMP = ACT.off + 8 * NTOK * 2
    KB = [alias(f"kb{i}", HT.off + i * 16384, 4 * NTOK, BF16) for i in range(2)]
    VB = [alias(f"vb{i}", HT.off + i * 16384 + 8192, 32 * 128, BF16) for i in range(2)]
    o_ = ACT.off
    KST = [alias(f"kst{i}", o_ + i * 2112, T, BF16) for i in range(2)]
    VST = [alias(f"vst{i}", o_ + 4224 + i * 1024, 512, BF16) for i in range(2)]
    SIG = [alias(f"sig{i}", o_ + 6272 + i * 4160, T, F32) for i in range(2)]
    assert 6272 + 2 * 4160 <= 16384
    EP = [alias(f"ep{i}", ATMP + i * 2048, 512, F32) for i in range(4)]
    assert 4 * 2048 <= 9216
    EP.append(alloc("ep4", 512, F32))
    PT = [alias(f"pt{i}", SG[0].off + i * 1024, 512, BF16) for i in range(4)]
    assert SG[1].off + 2112 - SG[0].off >= 4096
    CACC = alloc("cacc", NTOK, F32)
    CTMP = alloc("ctmp", NTOK, F32)
    HAL = alias("hal", SG[0].off, 4 * 8 * 32, F32)
    MH = alias("mh", ATMP + 8192, 8 * 32, F32)
    QP = alloc("qp", 2 * NTOK, BF16)
    print("SBUF used", off[0], "of", cap)

    BIG = [PBuf(nc, f"big{i}", 1024) for i in range(3)]
    SMALL = PBuf(nc, "small", 512)
    STAT = PBuf(nc, "stat", 512)
    cnt = {"big": 0, "small": 0, "slot": 0}

    def nbig():
        b = BIG[cnt["big"] % 3]
        cnt["big"] += 1
        return b

    def mm(out, lhsT, rhs, start, stop):
        P.add("pe", lambda e: e.matmul(out.ap, lhsT.ap, rhs.ap, start=start, stop=stop),
              reads=[lhsT, rhs], writes=[out])

    def act(out, in_, func, bias=None, scale=1.0, nosame=False):
        rd = [in_] + ([bias] if isinstance(bias, V) else [])
        if bias is None:
            P.add("act", lambda e: e.activation(out.ap, in_.ap, func, scale=scale), reads=rd, writes=[out], nosame=nosame)
        else:
            b = bias.ap if isinstance(bias, V) else bias
            P.add("act", lambda e: e.activation(out.ap, in_.ap, func, bias=b, scale=scale), reads=rd, writes=[out], nosame=nosame)

    def _s(x):
        return x.ap if isinstance(x, V) else x

    def tsc(out, in0, s1, s2, op0, op1=None, eng="dve"):
        rd = [in0] + [s for s in (s1, s2) if isinstance(s, V)]
        if op1 is None:
            P.add(eng, lambda e: e.tensor_scalar(out.ap, in0.ap, _s(s1), None, op0), reads=rd, writes=[out])
        else:
            P.add(eng, lambda e: e.tensor_scalar(out.ap, in0.ap, _s(s1), _s(s2), op0, op1), reads=rd, writes=[out])

    def stt(out, in0, sc, in1, op0, op1, eng="dve"):
        rd = [in0, in1] + ([sc] if isinstance(sc, V) else [])
        P.add(eng, lambda e: e.scalar_tensor_tensor(out.ap, in0.ap, _s(sc), in1.ap, op0, op1), reads=rd, writes=[out])

    def tt(out, in0, in1, op, eng="dve"):
        P.add(eng, lambda e: e.tensor_tensor(out.ap, in0.ap, in1.ap, op), reads=[in0, in1], writes=[out])

    def dma(eng, out, in_, reads, writes, chan):
        P.add(eng, lambda e: e.dma_start(out=out, in_=in_), reads=reads, writes=writes, chan=chan)

    stream = []
    for b, (c0, c1) in enumerate(FBLK):
        for s in range(c0 // 2, c1 // 2):
            stream.append((d_f1gu, s, WSLOT))
        for dg in range(4):
            stream.append((d_f1d, b * 4 + dg, (c1 - c0) * 512))
    W_IN0 = len(stream)
    for s in range(10):
        stream.append((d_win, s, WSLOT))
    W_OUT0 = len(stream)
    for s in range(4):
        stream.append((d_wout, s, WSLOT))
    F2_0 = len(stream)
    for b, (c0, c1) in enumerate(FBLK):
        for s in range(c0 // 2, c1 // 2):
            stream.append((d_f2gu, s, WSLOT))
        for dg in range(4):
            stream.append((d_f2d, b * 4 + dg, (c1 - c0) * 512))
    issued = [0]
    ag_after = {}

    WS3 = alias("ws3", X.off + 8 * 1056 * 4 + NH * NTOK * 2, WSLOT, BF16)
    ring4 = [WS[2], WS[0], WS[1], WS3]
    slot_of = []
    for j in range(len(stream)):
        if W_IN0 <= j < W_IN0 + 10:
            r_ = (j - W_IN0) % 4
            slot_of.append((ring4[r_], "ws3" if r_ == 3 else f"ws{(2, 0, 1)[r_]}"))
        elif j < W_IN0:
            slot_of.append((WS[j % 3], f"ws{j % 3}"))
        else:
            jj = j - (W_IN0 + 10)
            slot_of.append((WS[(jj + 1) % 3], f"ws{(jj + 1) % 3}"))
    assert W_IN0 % 3 == 2, W_IN0

    def wget(i):
        la = 3 if W_IN0 <= i < W_IN0 + 8 else 2
        while issued[0] < min(len(stream), i + la + 1):
            j = issued[0]
            h, idx, ne = stream[j]
            slot, chn = slot_of[j]
            prev = [p for p in range(j) if slot_of[p][0] is slot]
            assert not prev or prev[-1] < i, (i, j, prev[-1])
            dma("pool", slot.v(0, ne).ap, h.ap()[idx][:, 0:ne], [(h.name, idx, idx + 1)], [slot.v(0, ne)], chn)
            issued[0] += 1
            for th in ag_after.pop(j, []):
                th()
        return slot_of[i][0]

    dma("sp", TAB.v().ap, d_tab.ap(), [("tab_d", 0, 1)], [TAB.v()], "tab")
    for ti_, (c0_, c1_) in enumerate(((0, 512), (512, 1024), (1024, T))):
        for k in range(KC):
            dma("sp", X.v(k * T + c0_, k * T + c1_).ap, d_xT.ap()[:, k * T + c0_:k * T + c1_], [("xT_d", 0, 1)],
                [X.v(k * T + c0_, k * T + c1_)], f"xin{ti_}")
    P.add("dve", lambda e: e.memset(ONES.v().ap, 1.0), writes=[ONES.v()])
    P.add("dve", lambda e: e.memset(ONES16.v().ap, 0.0), writes=[ONES16.v()])
    P.add("dve", lambda e: e.memset(ONES16.v(0, 128, 0, 16).ap, 1.0), writes=[ONES16.v()])
    P.add("dve", lambda e: e.memset(QP.v().ap, 0.0), writes=[QP.v()])
    P.add("dve", lambda e: e.memset(KMETA.v().ap, 0.0), writes=[KMETA.v()])
    P.add("dve", lambda e: e.memset(VMETA.v().ap, 0.0), writes=[VMETA.v()])
    P.add("dve", lambda e: e.memset(BDG.v().ap, 0.0), writes=[BDG.v()])
    P.add("dve", lambda e: e.memset(BDG.v(0, 64, 0, 64).ap, 1.0), writes=[BDG.v()])
    P.add("dve", lambda e: e.memset(BDG.v(64, 128, 64, 128).ap, 1.0), writes=[BDG.v()])
    tsc(TB2.v(0, 64), TAB.v(0, 64), float(np.sqrt(D)), None, ALU.mult)
    tsc(TB2.v(64, 65), TAB.v(TGQ, TGQ + 1), 1.0, None, ALU.mult)
    tsc(TB2.v(65, 66), TAB.v(TGK, TGK + 1), 8.0, None, ALU.mult)
    tsc(TB2.v(66, 67), TAB.v(TGS, TGS + 1), float(np.sqrt(128.0) * (1.0 - LAM_INIT)), None, ALU.mult)
    tsc(TB2.v(68, 76), TAB.v(TGC, TGC + 8), 32.0, None, ALU.mult)
    tt(EP[0].v(0, 64), TAB.v(TLAM, TLAM + 64), TAB.v(TLAM + 64, TLAM + 128), ALU.mult)
    tt(EP[0].v(64, 128), TAB.v(TLAM + 128, TLAM + 192), TAB.v(TLAM + 192, TLAM + 256), ALU.mult)
    P.add("dve", lambda e: e.reduce_sum(LAMT.v(0, 1).ap, EP[0].v(0, 64).ap, mybir.AxisListType.X),
          reads=[EP[0].v(0, 64)], writes=[LAMT.v(0, 1)])
    P.add("dve", lambda e: e.reduce_sum(LAMT.v(1, 2).ap, EP[0].v(64, 128).ap, mybir.AxisListType.X),
          reads=[EP[0].v(64, 128)], writes=[LAMT.v(1, 2)])
    act(LAMT.v(2, 4), LAMT.v(0, 2), AF.Exp)
    tt(LAMT.v(4, 5), LAMT.v(3, 4), LAMT.v(2, 3), ALU.subtract)
    tsc(LAMT.v(5, 6), LAMT.v(4, 5), -LAM_INIT, None, ALU.add)
    NEGLAM = LAMT.v(5, 6)

    for i_, n_ in enumerate((D, 64, 128, 1024)):
        P.add("dve", lambda e, i_=i_, n_=n_: e.memset(TB2.v(80 + i_, 81 + i_).ap, float(n_ * EPS)), writes=[TB2.v(80 + i_, 81 + i_)])

    def rsqrt_eps(out, ss, epscol):
        act(out, ss, AF.Ln, bias=TB2.v(epscol, epscol + 1))
        act(out, out, AF.Exp, scale=-0.5)

    def tiles(ntok):
        tl = [(0, 512), (512, 1024)]
        if ntok > 1024:
            tl.append((1024, ntok))
        return tl

    def rmsnorm_to_hT(gcol0, ntok):
        for (c0, c1) in tiles(ntok):
            w = c1 - c0
            for k in range(KC):
                sq = SQ[k % 2]
                act(sq.v(0, w), X.v(k * T + c0, k * T + c1), AF.Square)
                mm(STAT.v(0, w), ONES.v(), sq.v(0, w), k == 0, k == KC - 1)
            rsqrt_eps(RS.v(c0, c1), STAT.v(0, w), 80)
            for k in range(KC):
                stt(HT.v(k * T + c0, k * T + c1), X.v(k * T + c0, k * T + c1), TB2.v(gcol0 + k, gcol0 + k + 1),
                    RS.v(c0, c1), ALU.mult, ALU.mult)

    def proj_group(slot, colfn, ntok, sm=None):
        big = nbig()
        for k in range(KC):
            lhsT = slot.v(colfn(k), colfn(k) + 128)
            for (c0, c1) in tiles(ntok):
                out = big.v(c0, c1) if c0 < 1024 else sm
                mm(out, lhsT, HT.v(k * T + c0, k * T + c1), k == 0, k == KC - 1)
        return big

    def v3of(buf, n, inner, a, b):
        t = buf.v(0, n * inner)
        return V(t.ap.rearrange("p (c t) -> p c t", t=inner)[:, :, a:b], t.key, t.lo, t.hi)

    def g16(view, n):
        return V(view.ap.rearrange("p (c t) -> p c t", t=16), view.key, view.lo, view.hi)

    def ffn(si0, ntok, gcol0):
        rmsnorm_to_hT(gcol0, ntok)
        meta = ntok > 1024
        si = si0
        for b, (f0, f1) in enumerate(FBLK):
            nfc = f1 - f0
            for s in range(f0 // 2, f1 // 2):
                slot = wget(si)
                si += 1
                for fl in range(2):
                    fc = 2 * s + fl - f0
                    bg = proj_group(slot, lambda k, fl=fl: (0 * 16 + k) * 256 + fl * 128, ntok,
                                    SMALL.v(fc * 16, fc * 16 + 16))
                    sgb = SG[(2 * s + fl) % 2]
                    act(sgb.v(0, 1024), bg.v(0, 1024), AF.Silu)
                    bu = proj_group(slot, lambda k, fl=fl: (1 * 16 + k) * 256 + fl * 128, ntok,
                                    SMALL.v(192 + fc * 16, 192 + fc * 16 + 16))
                    tt(ACT.v(fc * T, fc * T + 1024), bu.v(0, 1024), sgb.v(0, 1024), ALU.mult)
            if meta:
                act(SQ[0].v(0, nfc * 16), SMALL.v(0, nfc * 16), AF.Silu)
                tt(v3of(ACT, nfc, T, 1024, T), g16(SMALL.v(192, 192 + nfc * 16), nfc), g16(SQ[0].v(0, nfc * 16), nfc), ALU.mult)
            for dg in range(4):
                slot = wget(si)
                si += 1
                for dl in range(4):
                    dc = dg * 4 + dl
                    big = nbig()
                    for fl in range(nfc):
                        lhsT = slot.v(fl * 512 + dl * 128, fl * 512 + dl * 128 + 128)
                        for (c0, c1) in tiles(ntok):
                            out = big.v(c0, c1) if c0 < 1024 else STAT.v(dc * 16, dc * 16 + 16)
                            mm(out, lhsT, ACT.v(fl * T + c0, fl * T + c1), fl == 0, fl == nfc - 1)
                    stt(X.v(dc * T, dc * T + 1024), big.v(0, 1024), 0.5, X.v(dc * T, dc * T + 1024), ALU.mult, ALU.add)
            if meta:
                xm = v3of(X, KC, T, 1024, T)
                stt(xm, g16(STAT.v(0, 256), 16), 0.5, xm, ALU.mult, ALU.add)
        return si

    def dbg_out(view, n):
        P.add("sp", lambda e: e.dma_start(out=d_dbg.ap()[:, 0:n], in_=view.ap), reads=[view], writes=[("dbg_d", 0, 1)], chan="dbg")

    si = ffn(0, T, TG1)
    assert si == W_IN0
    def finish_dbg(view, n):
        dbg_out(view, n)
        P.add("sp", lambda e: e.nop(), reads=[("dbg_d", 0, 1)])
        with ExitStack() as st:
            P.emit(nc, st)
        return nc

    if debug is not None and debug[0] == "ffn1":
        return finish_dbg(X.v(), KC * T)

    rmsnorm_to_hT(TGM, T)
    for k in range(KC):
        dma("sp", d_xsp.ap()[:, k * NTOK:(k + 1) * NTOK], X.v(k * T, k * T + NTOK).ap,
            [X.v(k * T, k * T + NTOK)], [("xsp_d", k, k + 1)], "xsp")

    def qknorm(dst, src, n, gcol, sq):
        act(sq.v(0, n), src, AF.Square)
        big = nbig()
        for c0 in range(0, n, 512):
            c1 = min(n, c0 + 512)
            mm(big.v(c0, c1), BDG.v(), sq.v(c0, c1), True, True)
        rsqrt_eps(RS.v(0, n), big.v(0, n), 81)
        stt(dst, src, TB2.v(gcol, gcol + 1), RS.v(0, n), ALU.mult, ALU.mult)

    si = W_IN0
    KST8 = [alias(f"kst8_{i}", QT.off + i * 2048, NTOK, BF16) for i in range(8)]
    VST18 = [alias(f"vst18_{i}", U.off + i * 1024, 512, BF16) for i in range(16)]
    for sl in range(2):
        slot = wget(si)
        si += 1
        for hl in range(4):
            hh = sl * 4 + hl
            bk = proj_group(slot, lambda k, hl=hl: k * 512 + hl * 128, T, SMALL.v(hl * 16, hl * 16 + 16))
            kst = KST8[hh]
            qknorm(kst.v(0, NTOK), bk.v(0, NTOK), NTOK, 65, SQ[hh % 2])
            dma("sp", d_kin[sl].ap()[hl * 128:(hl + 1) * 128, :], kst.v(0, NTOK).ap, [kst.v(0, NTOK)], [("kin_d", hh, hh + 1)], f"kinS{sl}")
        act(SQ[0].v(0, 64), SMALL.v(0, 64), AF.Square)
        mm(STAT.v(0, 64), BDG.v(), SQ[0].v(0, 64), True, True)
        rsqrt_eps(RS.v(0, 64), STAT.v(0, 64), 81)
        for hl in range(4):
            hh = sl * 4 + hl
            stt(KMETA.v(hh * 128, hh * 128 + 16), SMALL.v(hl * 16, hl * 16 + 16), TB2.v(65, 66), RS.v(hl * 16, hl * 16 + 16), ALU.mult, ALU.mult)
        ag_after.setdefault(W_IN0 + 4 + 2 * sl, []).append(
            lambda sl=sl: P.add("pool", lambda e: e.collective_compute("AllGather", ALU.bypass, replica_groups=GROUPS,
                                                                      ins=[d_kin[sl].ap().opt()], outs=[d_kout[sl].ap().opt()]),
                                reads=[("kin_d", sl * 4, sl * 4 + 4)], writes=[("kout_d", sl, sl + 1)], chan=f"agk{sl}", inc=1))
    for sl in range(2):
        slot = wget(si)
        si += 1
        for ti in range(9):
            nt = 128 if ti < 8 else 16
            big = nbig()
            out = big.v(0, 512, 0, nt)
            for k in range(KC):
                mm(out, HT.v(k * T + ti * 128, k * T + ti * 128 + nt), slot.v(k * 512, k * 512 + 512), k == 0, k == KC - 1)
            if ti < 8:
                vst = VST18[sl * 8 + ti]
                act(vst.v(), out, AF.Copy)
                dma("sp", d_vin[sl].ap()[ti * 128:(ti + 1) * 128, :], vst.v().ap,
                    [vst.v()], [("vin_d", sl * 8 + ti, sl * 8 + ti + 1)], f"vinS{sl}")
            else:
                act(VMETA.v(sl * 512, sl * 512 + 512, 0, 16), out, AF.Copy)
        ag_after.setdefault(W_IN0 + 8 + 2 * sl, []).append(
            lambda sl=sl: P.add("pool", lambda e: e.collective_compute("AllGather", ALU.bypass, replica_groups=GROUPS,
                                                                      ins=[d_vin[sl].ap().opt()], outs=[d_vout[sl].ap().opt()]),
                                reads=[("vin_d", sl * 8, sl * 8 + 8)], writes=[("vout_d", sl, sl + 1)], chan=f"agv{sl}", inc=1))
    for sl in range(4):
        slot = wget(si)
        si += 1
        for jl in range(2):
            j = sl * 2 + jl
            ba = proj_group(slot, lambda k, jl=jl: k * 512 + (2 * jl) * 128, T, SMALL.v((2 * jl) * 16, (2 * jl) * 16 + 16))
            bgt = proj_group(slot, lambda k, jl=jl: k * 512 + (2 * jl + 1) * 128, T, SMALL.v((2 * jl + 1) * 16, (2 * jl + 1) * 16 + 16))
            sig = SIG[j % 2]
            act(sig.v(0, NTOK), bgt.v(0, NTOK), AF.Sigmoid)
            tt(U.v(j * UW + 32, j * UW + 32 + NTOK), ba.v(0, NTOK), sig.v(0, NTOK), ALU.mult)
        for jl in range(2):
            j = sl * 2 + jl
            act(SIG[0].v(0, 16), SMALL.v((2 * jl + 1) * 16, (2 * jl + 1) * 16 + 16), AF.Sigmoid)
            tt(UMETA.v(j * 16, j * 16 + 16), SMALL.v((2 * jl) * 16, (2 * jl) * 16 + 16), SIG[0].v(0, 16), ALU.mult)
    u3 = V(U.h[:, 0:8 * UW].rearrange("p (c t) -> p c t", t=UW)[:, :, 32 + NTOK - 32:32 + NTOK], "sb", U.off, U.off + 8 * UW * 4)
    dma("sp", d_hin.ap().rearrange("(j p) t -> p j t", p=128), u3.ap, [u3], [("hin_d", 0, 1)], "hin")
    ag_after.setdefault(W_IN0 + 11, []).append(
        lambda: P.add("pool", lambda e: e.collective_compute("AllGather", ALU.bypass, replica_groups=GROUPS,
                                                             ins=[d_hin.ap().opt()], outs=[d_hout.ap().opt()]),
                      reads=[("hin_d", 0, 1)], writes=[("hout_d", 0, 1)], chan="agh", inc=1))
    for sl in range(2):
        slot = wget(si)
        si += 1
        for hl in range(4):
            hh = sl * 4 + hl
            bq = proj_group(slot, lambda k, hl=hl: k * 512 + hl * 128, NTOK)
            qknorm(QT.v(hh * NTOK, (hh + 1) * NTOK), bq.v(0, NTOK), NTOK, 64, SQ[hh % 2])
    assert si == W_OUT0
    assert not ag_after, list(ag_after)
    if debug is not None and debug[0] == "win":
        return finish_dbg(QT.v(), NH * NTOK)

    dma("sp", HAL.v3(0, 4 * 8 * 32, 32).ap, d_hout.ap().rearrange("(c p) t -> p c t", p=128),
        [("hout_d", 0, 1)], [HAL.v()], "hal")
    P.add("dve", lambda e: e.memset(MH.v().ap, 0.0), writes=[MH.v()])
    mh3 = V(MH.h[:, :].rearrange("p (c t) -> p c t", t=32)[:, :, 16:32], "sb", MH.off, MH.off + 8 * 32 * 4)
    P.add("dve", lambda e: e.tensor_copy(mh3.ap, g16(UMETA.v(), 8).ap), reads=[UMETA.v()], writes=[mh3])
    uh = V(U.h[:, 0:8 * UW].rearrange("p (c t) -> p c t", t=UW)[:, :, 0:32], "sb", U.off, U.off + 8 * UW * 4)
    mhv = V(MH.h[:, :].rearrange("p (c t) -> p c t", t=32), "sb", MH.off, MH.off + 8 * 32 * 4)
    tsc(uh, mhv, TAB.v(TSEL + 4, TSEL + 5), None, ALU.mult)
    for r in range(4):
        hr = V(HAL.h[:, r * 256:(r + 1) * 256].rearrange("p (c t) -> p c t", t=32), "sb", HAL.off + r * 1024, HAL.off + (r + 1) * 1024)
        stt(uh, hr, TAB.v(TSEL + r, TSEL + r + 1), uh, ALU.mult, ALU.add)

    if debug is not None and debug[0] == "halo":
        return finish_dbg(U.v(), 8 * UW)

    def conv_chunk(j, part):
        ub = j * UW
        if part == 0:
            tsc(CACC.v(), U.v(ub + 2, ub + 2 + NTOK), TAB.v(TCW + j * 31, TCW + j * 31 + 1), TAB.v(TCB + j, TCB + j + 1), ALU.mult, ALU.add)
            taps = range(1, 16)
        else:
            taps = range(16, 30)
        for jt in taps:
            stt(CACC.v(), U.v(ub + 2 + jt, ub + 2 + jt + NTOK), TAB.v(TCW + j * 31 + jt, TCW + j * 31 + jt + 1), CACC.v(), ALU.mult, ALU.add)
        if part == 0:
            return
        y = U.v(ub + 32, ub + 32 + NTOK)
        stt(y, y, TAB.v(TCW + j * 31 + 30, TCW + j * 31 + 31), CACC.v(), ALU.mult, ALU.add)
        if j == 0:
            tt(CTMP.v(), y, y, ALU.mult)
        else:
            tt(CACC.v(), y, y, ALU.mult)
            tt(CTMP.v(), CTMP.v(), CACC.v(), ALU.add)

    def conv_finish():
        sq = SQ[0]
        P.add("dve", lambda e: e.tensor_copy(sq.v(0, NTOK).ap, CTMP.v().ap), reads=[CTMP.v()], writes=[sq.v(0, NTOK)])
        for c0 in (0, 512):
            mm(STAT.v(), ONES.v(), sq.v(c0, c0 + 512), True, True)
            rsqrt_eps(CTMP.v(c0, c0 + 512), STAT.v(), 83)
        for j in range(8):
            y = U.v(j * UW + 32, j * UW + 32 + NTOK)
            stt(CACC.v(), y, TB2.v(68 + j, 69 + j), CTMP.v(), ALU.mult, ALU.mult)
            act(CATC.v(j * NTOK, (j + 1) * NTOK), CACC.v(), AF.Silu)

    kg = [d_kout[i].ap().rearrange("(r h p) t -> h p r t", r=4, h=4, p=128) for i in range(2)]
    vg = [d_vout[i].ap().rearrange("(n p) c -> p n c", p=128) for i in range(2)]
    SPAIR = [BIG[0], BIG[1]]
    O_ = [BIG[2].v(0, 512), BIG[2].v(512, 1024)]
    SUM_ = [SMALL.v(), STAT.v()]
    PTP = [alias(f"ptp{i}", SG[0].off + i * 2048, 1024, BF16) for i in range(2)]

    def load_kv(hh):
        kb, vb = KB[hh % 2], VB[hh % 2]
        sl, hl = hh // 4, hh % 4
        dma("sp", kb.v3(0, 4 * NTOK, NTOK).ap, kg[sl][hl], [("kout_d", sl, sl + 1)], [kb.v()], f"kb{hh % 2}")
        dma("sp", vb.v3(0, 32 * 128, 128).ap, vg[sl][:, :, hl * 128:(hl + 1) * 128], [("vout_d", sl, sl + 1)], [vb.v()], f"vb{hh % 2}")

    SQF = alias("sqf", SQ[0].off, 512, F32)

    def qp_copy(hh):
        P.add("dve", lambda e: e.tensor_copy(QP.v(0, NTOK, 0, 64).ap, QT.v(hh * NTOK, (hh + 1) * NTOK, 0, 64).ap),
              reads=[QT.v(hh * NTOK, (hh + 1) * NTOK)], writes=[QP.v(0, NTOK)])
        P.add("dve", lambda e: e.tensor_copy(QP.v(NTOK, 2 * NTOK, 64, 128).ap, QT.v(hh * NTOK, (hh + 1) * NTOK, 64, 128).ap),
              reads=[QT.v(hh * NTOK, (hh + 1) * NTOK)], writes=[QP.v(NTOK, 2 * NTOK)])

    def make_tiles(hh, g):
        kb, vb = KB[hh % 2], VB[hh % 2]
        tilesl = [(KMETA.v(hh * 128, hh * 128 + 128), VMETA.v(hh * 128, hh * 128 + 128), ONES16.v(), [(0, 128, 0, 512, None)])]
        for r in range(4):
            for kt in range(8):
                kv = kb.v(r * NTOK + kt * 128, r * NTOK + kt * 128 + 128)
                vv = vb.v((r * 8 + kt) * 128, (r * 8 + kt) * 128 + 128)
                if kt < 4 * g:
                    rects = [(0, 128, 0, 512, TMB + 4 + r)]
                elif kt >= 4 * g + 4:
                    rects = [(0, 128, 0, 512, TMB + r)]
                else:
                    pp = kt - 4 * g
                    rects = []
                    if pp > 0:
                        rects.append((0, 128, 0, 128 * pp, TMB + r))
                    rects.append((0, 128, 128 * pp, 128 * pp + 64, TSA + r))
                    rects.append((0, 128, 128 * pp + 64, 512, TMB + 4 + r))
                tilesl.append((kv, vv, ONES.v(), rects))
        return tilesl

    def issue_s_of(tilesl, g, t):
        kv = tilesl[t][0]
        for c in range(2):
            mm(SPAIR[t % 2].v(c * 512, c * 512 + 512), kv, QP.v(c * NTOK + g * 512, c * NTOK + g * 512 + 512), True, True)

    pre_issued = set()

    def attn_head(hh):
        for g in range(2):
            tilesl = make_tiles(hh, g)
            n = len(tilesl)

            def issue_s(t):
                issue_s_of(tilesl, g, t)

            def pair3(buf, c0, c1):
                t_ = buf.v(0, 1024)
                return V(t_.ap.rearrange("p (c t) -> p c t", t=512)[:, :, c0:c1], t_.key, t_.lo, t_.hi)

            if (hh, g) not in pre_issued:
                issue_s(0)
                issue_s(1)
            for t in range(n):
                kv, vv, ov, rects = tilesl[t]
                sp_, pt = SPAIR[t % 2], PTP[t % 2]
                for ri, (p0, p1, c0, c1, bcol) in enumerate(rects):
                    bias = None if bcol is None else TAB.v(bcol, bcol + 1)
                    act(pair3(pt, c0, c1), pair3(sp_, c0, c1), AF.Exp, bias=bias, nosame=(ri > 0))
                for c in range(2):
                    mm(O_[c], vv, pt.v(c * 512, c * 512 + 512), t == 0, t == n - 1)
                if t + 2 < n:
                    issue_s(t + 2)
                for c in range(2):
                    mm(SUM_[c], ov, pt.v(c * 512, c * 512 + 512), t == 0, t == n - 1)
            if g == 1 and hh + 1 < NH:
                qp_copy(hh + 1)
            nxt = (hh, 1) if g == 0 else ((hh + 1, 0) if hh + 1 < NH else None)
            if nxt is not None:
                tl2 = make_tiles(*nxt)
                issue_s_of(tl2, nxt[1], 0)
                issue_s_of(tl2, nxt[1], 1)
                pre_issued.add(nxt)
            n_ = hh * 2 + g
            stmp = [RS.v(0, 512), SQF.v()]
            for c in range(2):
                P.add("dve", lambda e, c=c: e.tensor_copy(EP[2 + c].v().ap, O_[c].ap), reads=[O_[c]], writes=[EP[2 + c].v()])
            for c in range(2):
                P.add("dve", lambda e, c=c: e.tensor_copy(stmp[c].ap, SUM_[c].ap), reads=[SUM_[c]], writes=[stmp[c]])
            if pend[0] is not None:
                attn_finalize(pend[0])
            for c in range(2):
                P.add("dve", lambda e, c=c: e.reciprocal(stmp[c].ap, stmp[c].ap), reads=[stmp[c]], writes=[stmp[c]])
                tt(EP[2 + c].v(), EP[2 + c].v(), stmp[c], ALU.mult)
            ob = OB[n_ % 2]
            stt(ob.v(), EP[3].v(), NEGLAM, EP[2].v(), ALU.mult, ALU.add)
            tt(SQ[1].v((n_ % 2) * 512, (n_ % 2) * 512 + 512), ob.v(), ob.v(), ALU.mult)
            pend[0] = (hh, g, n_)
            if g == 0:
                conv_chunk(hh, 1)

    OB = [EP[4], alloc("ob1", 512, F32)]
    pend = [None]

    def attn_finalize(p):
        hh, g, n_ = p
        mm(O_[0], ONES.v(), SQ[1].v((n_ % 2) * 512, (n_ % 2) * 512 + 512), True, True)
        act(EP[0].v(), O_[0], AF.Ln, bias=TB2.v(82, 83))
        act(EP[0].v(), EP[0].v(), AF.Exp, scale=-0.5)
        stt(CATA.v(hh * NTOK + g * 512, hh * NTOK + g * 512 + 512), OB[n_ % 2].v(), TB2.v(66, 67), EP[0].v(), ALU.mult, ALU.mult)

    load_kv(0)
    qp_copy(0)
    for hh in range(NH):
        if hh + 1 < NH:
            load_kv(hh + 1)
        conv_chunk(hh, 0)
        attn_head(hh)
        if debug is not None and debug[0] == "attn0":
            return finish_dbg(CATA.v(), NH * NTOK)
    attn_finalize(pend[0])
    conv_finish()

    WTMP = alias("wtmp", HT.off, 4 * NTOK, F32)

    def reload_x(k):
        dma("sp", X.v(k * T, k * T + NTOK).ap, d_xsp.ap()[:, k * NTOK:(k + 1) * NTOK],
            [("xsp_d", k, k + 1)], [X.v(k * T, k * T + NTOK)], "xrl" if k < 12 else f"xrlb{k}")

    for k in range(12):
        reload_x(k)
    si = W_OUT0
    for dg in range(4):
        slot = wget(si)
        si += 1
        for dl in range(4):
            dc = dg * 4 + dl
            big = nbig()
            for c in range(16):
                lhsT = slot.v(c * 512 + dl * 128, c * 512 + dl * 128 + 128)
                src = CATA if c < 8 else CATC
                cc = c % 8
                for (c0, c1) in tiles(NTOK):
                    mm(big.v(c0, c1), lhsT, src.v(cc * NTOK + c0, cc * NTOK + c1), c == 0, c == 15)
            if dc < 12:
                tt(X.v(dc * T, dc * T + NTOK), big.v(0, NTOK), X.v(dc * T, dc * T + NTOK), ALU.add)
            else:
                act(WTMP.v((dc - 12) * NTOK, (dc - 11) * NTOK), big.v(0, NTOK), AF.Copy)
    for k in range(12, 16):
        reload_x(k)
        tt(X.v(k * T, k * T + NTOK), WTMP.v((k - 12) * NTOK, (k - 11) * NTOK), X.v(k * T, k * T + NTOK), ALU.add)
    assert si == F2_0

    si = ffn(F2_0, NTOK, TG2)
    assert si == len(stream)
    for (c0, c1) in tiles(NTOK):
        w = c1 - c0
        for k in range(KC):
            sq = SQ[k % 2]
            act(sq.v(0, w), X.v(k * T + c0, k * T + c1), AF.Square)
            mm(STAT.v(0, w), ONES.v(), sq.v(0, w), k == 0, k == KC - 1)
        rsqrt_eps(RS.v(c0, c1), STAT.v(0, w), 80)
        for k in range(KC):
            o = EP[k % 4]
            stt(o.v(0, w), X.v(k * T + c0, k * T + c1), TB2.v(TGF + k, TGF + k + 1), RS.v(c0, c1), ALU.mult, ALU.mult)
            dma("sp", d_out.ap()[:, k * NTOK + c0:k * NTOK + c1], o.v(0, w).ap, [o.v(0, w)], [("out_d", 0, 1)], f"out{k % 4}")
    P.add("sp", lambda e: e.nop(), reads=[("out_d", 0, 1)])
    with ExitStack() as st:
        P.emit(nc, st)
    return nc


def _prep_gu(wg, wu):
    out = np.empty((22, 128, 2, 16, 256), np.float32)
    for which, w in enumerate((wg, wu)):
        out[:, :, which] = w.reshape(16, 128, 22, 256).transpose(2, 1, 0, 3)
    return out.reshape(22, 128, WSLOT)


def _prep_d(wd):
    out = np.zeros((16, 128, 12, 512), np.float32)
    for b, (f0, f1) in enumerate(FBLK):
        blk = wd[f0 * 128:f1 * 128].reshape(f1 - f0, 128, 4, 512)
        out[b * 4:(b + 1) * 4, :, :f1 - f0] = blk.transpose(2, 1, 0, 3)
    return out.reshape(16, 128, 12 * 512)


def _prep_win(w_in):
    order = []
    order += [8 + h for h in range(8)]
    order += [16 + h for h in range(8)]
    for i in range(4):
        order += [24 + 2 * i, 32 + 2 * i, 24 + 2 * i + 1, 32 + 2 * i + 1]
    order += [h for h in range(8)]
    cols = np.concatenate([np.arange(c * 128, (c + 1) * 128) for c in order])
    w = w_in[:, cols]
    return np.ascontiguousarray(w.reshape(16, 128, 10, 512).transpose(2, 1, 0, 3)).reshape(10, 128, WSLOT)


def _prep_wout(w_out):
    return np.ascontiguousarray(w_out.reshape(16, 128, 4, 512).transpose(2, 1, 0, 3)).reshape(4, 128, WSLOT)


def _col(v):
    return np.ascontiguousarray(v.reshape(-1, 128).T)


def make_in_maps(inp):
    f32 = np.float32
    x = np.asarray(inp["x"], f32)
    meta = np.asarray(inp["meta_tokens"], f32)
    shared = {
        "f1gu": _prep_gu(np.asarray(inp["ffn1_w_gate"], f32)[0], np.asarray(inp["ffn1_w_up"], f32)[0]),
        "f1d": _prep_d(np.asarray(inp["ffn1_w_down"], f32)[0]),
        "f2gu": _prep_gu(np.asarray(inp["ffn2_w_gate"], f32)[0], np.asarray(inp["ffn2_w_up"], f32)[0]),
        "f2d": _prep_d(np.asarray(inp["ffn2_w_down"], f32)[0]),
        "win": _prep_win(np.asarray(inp["w_in"], f32)[0]),
        "wout": _prep_wout(np.asarray(inp["w_out"], f32)[0]),
    }
    tab0 = np.zeros((128, TABN), f32)
    tab0[:, TG1:TG1 + 16] = _col(np.asarray(inp["ffn1_norm_g"], f32)[0])
    tab0[:, TGM:TGM + 16] = _col(np.asarray(inp["mix_norm_g"], f32)[0])
    tab0[:, TG2:TG2 + 16] = _col(np.asarray(inp["ffn2_norm_g"], f32)[0])
    tab0[:, TGF:TGF + 16] = _col(np.asarray(inp["final_norm_g"], f32)[0])
    tab0[:, TGQ] = np.tile(np.asarray(inp["q_norm_g"], f32)[0], 2)
    tab0[:, TGK] = np.tile(np.asarray(inp["k_norm_g"], f32)[0], 2)
    tab0[:, TGS] = np.asarray(inp["attn_subln_g"], f32)[0]
    tab0[:, TCB:TCB + 8] = _col(np.asarray(inp["conv_b"], f32)[0])
    tab0[:, TGC:TGC + 8] = _col(np.asarray(inp["conv_norm_g"], f32)[0])
    cw = np.asarray(inp["conv_w"], f32)[0][:, 0, :]
    tab0[:, TCW:TCW + 248] = cw.reshape(31, 8, 128).transpose(2, 1, 0).reshape(128, 248)
    lam = np.concatenate([np.asarray(inp[k], f32)[0] for k in ("lambda_q1", "lambda_k1", "lambda_q2", "lambda_k2")])
    tab0[:, TLAM:TLAM + 256] = lam[None, :]
    in_maps = []
    for c in range(8):
        b, j = c // 4, c % 4
        tok = np.concatenate([x[b, j * NTOK:(j + 1) * NTOK], meta], axis=0)
        xT = np.ascontiguousarray(tok.reshape(T, KC, 128).transpose(2, 1, 0)).reshape(128, KC * T)
        tab = tab0.copy()
        sel = np.zeros(5, f32)
        if j == 0:
            sel[4] = 1.0
        else:
            sel[j - 1] = 1.0
        tab[:, TSEL:TSEL + 5] = sel[None, :]
        mb = np.zeros(8, f32)
        for r in range(4):
            mb[r] = 0.0 if r < j else NEG
            mb[4 + r] = 0.0 if r <= j else NEG
        tab[:, TMB:TMB + 8] = mb[None, :]
        for r in range(4):
            tab[:64, TSA + r] = mb[4 + r]
            tab[64:, TSA + r] = mb[r]
        m = dict(shared)
        m["xT"] = xT
        m["tab"] = tab
        in_maps.append(m)
    return in_maps


DEBUG = None


def kernel(**inputs):
    in_maps = make_in_maps(inputs)
    if DEBUG is not None:
        nc = build_program(DEBUG)
        res = run_bass_kernel_spmd(nc, in_maps, core_ids=list(range(8)))
        return [r["dbg"] for r in res.results]
    nc = build_program()
    res = run_bass_kernel_spmd(nc, in_maps, core_ids=list(range(8)))
    out = np.empty((2, 4 * NTOK, D), np.float32)
    for c in range(8):
        b, j = c // 4, c % 4
        oT = np.asarray(res.results[c]["outT"], np.float32).reshape(128, KC, NTOK)
        out[b, j * NTOK:(j + 1) * NTOK] = oT.transpose(2, 1, 0).reshape(NTOK, D)
    return out
```
